# Optimizing a Trainium2 kernel written in Bass

```python
import math
import jax, jax.numpy as jnp
from jax import lax
import numpy as np

D_MODEL = 2048
BATCH = 1
SEQ = 8192
DEPTH = 4
DEC_BATCH = 8
DEC_SEQ = 2048
PAST_LEN = 128

N_EVEN = (DEPTH + 1) // 2
N_ODD = DEPTH // 2
EPS = 1e-6
D_FF = 5632
POOL_WIDTH = D_MODEL // 2
POOL_WINDOWS = (2, 4, 8, 16)
N_POOL_GROUPS = len(POOL_WINDOWS)
POOL_GROUP = POOL_WIDTH // N_POOL_GROUPS
HYENA_WIDTH = D_MODEL - POOL_WIDTH
HYENA_EMB_BANDS = 16
HYENA_EMB_DIM = 1 + 2 * HYENA_EMB_BANDS
HYENA_FILTER_ORDER = 64
HYENA_FAST_DECAY = 0.3
HYENA_SLOW_DECAY = 1.5
HYENA_TARGET = 1e-2
AB_IN = POOL_WIDTH + 3 * HYENA_WIDTH
N_HEADS = 16
N_KV_HEADS = 4
HEAD_DIM = 128
GQA_GROUP = N_HEADS // N_KV_HEADS
WINDOW = 128
BLOCK = 128
QKV_OUT = (N_HEADS + 2 * N_KV_HEADS) * HEAD_DIM
N_BUCKETS = 32
MAX_DISTANCE = 128

kernel_name = "hybrid_pool_hyena_swa_encoder"


def rmsnorm(x, g):
    xf = x.astype(jnp.float32)
    y = xf * lax.rsqrt(jnp.mean(xf * xf, axis=-1, keepdims=True) + EPS)
    return (y * g.astype(jnp.float32)).astype(x.dtype)


def swiglu(h, w_in, w_out):
    gate, up = jnp.split(h @ w_in, 2, axis=-1)
    return (jax.nn.silu(gate) * up) @ w_out


def multiscale_pool(u, w_grp, scale):
    B, L, _ = u.shape
    uf = u.astype(jnp.float32)
    cs = jnp.concatenate([jnp.zeros((B, 1, POOL_WIDTH), jnp.float32), jnp.cumsum(uf, axis=1)], axis=1)
    t = jnp.arange(L)
    outs = []
    for g, w in enumerate(POOL_WINDOWS):
        sl = slice(g * POOL_GROUP, (g + 1) * POOL_GROUP)
        lo = jnp.clip(t - w // 2, 0, L)
        hi = jnp.clip(t + w // 2, 0, L)
        csg = cs[..., sl]
        s = jnp.take(csg, hi, axis=1) - jnp.take(csg, lo, axis=1)
        mean = s / (hi - lo).astype(jnp.float32)[None, :, None]
        outs.append(mean - uf[..., sl])
    p = jnp.stack(outs, axis=2).astype(u.dtype)
    y = jnp.einsum('blgc,gcd->blgd', p, w_grp).reshape(B, L, POOL_WIDTH)
    return y * scale


def hyena_filters(L, w1, b1, w2, b2, w3, b3, w4, b4, freq):
    f32 = jnp.float32
    t = jnp.linspace(0.0, 1.0, L, dtype=f32)[:, None]
    w = 2.0 * math.pi * jnp.arange(L, dtype=f32)[:, None] / L
    f = jnp.linspace(1e-4, HYENA_EMB_BANDS - 1, HYENA_EMB_BANDS, dtype=f32)[None, :]
    z = jnp.concatenate([t, jnp.cos(f * w), -jnp.sin(f * w)], axis=-1)
    fr = freq.astype(f32)
    h = jnp.sin(fr * (z @ w1.astype(f32) + b1.astype(f32)))
    h = jnp.sin(fr * (h @ w2.astype(f32) + b2.astype(f32)))
    h = jnp.sin(fr * (h @ w3.astype(f32) + b3.astype(f32)))
    h = h @ w4.astype(f32) + b4.astype(f32)
    min_decay = math.log(HYENA_TARGET) / HYENA_SLOW_DECAY
    max_decay = math.log(HYENA_TARGET) / HYENA_FAST_DECAY
    deltas = jnp.linspace(min_decay, max_decay, HYENA_WIDTH, dtype=f32)
    decay = jnp.exp(-t * jnp.abs(deltas)[None, :])
    h_f = h[:, :HYENA_WIDTH] * decay
    h_b = h[:, HYENA_WIDTH:] * decay
    k = jnp.concatenate([h_f, jnp.zeros((1, HYENA_WIDTH), f32), h_b[:0:-1]], axis=0)
    return k / jnp.sum(jnp.abs(k), axis=0, keepdims=True)


def hyena(u, conv_w, conv_b, w1, b1, w2, b2, w3, b3, w4, b4, freq, d_bias):
    B, L, _ = u.shape
    up = jnp.pad(u, ((0, 0), (1, 1), (0, 0)))
    uc = up[:, :-2] * conv_w[0] + up[:, 1:-1] * conv_w[1] + up[:, 2:] * conv_w[2] + conv_b
    x0, x1, v = jnp.split(uc, 3, axis=-1)
    k = hyena_filters(L, w1, b1, w2, b2, w3, b3, w4, b4, freq)
    v = (v * x1).astype(jnp.float32)
    vf = jnp.fft.rfft(v, n=2 * L, axis=1)
    kf = jnp.fft.rfft(k, axis=0)
    y = jnp.fft.irfft(vf * kf[None], n=2 * L, axis=1)[:, :L] + v * d_bias.astype(jnp.float32)
    return (y * x0.astype(jnp.float32)).astype(u.dtype)


def t5_buckets(rel):
    nb = N_BUCKETS // 2
    max_exact = nb // 2
    ret = (rel > 0).astype(jnp.int32) * nb
    n = jnp.abs(rel)
    large = max_exact + (jnp.log(jnp.maximum(n, 1).astype(jnp.float32) / max_exact)
                         / math.log(MAX_DISTANCE / max_exact) * (nb - max_exact)).astype(jnp.int32)
    large = jnp.minimum(large, nb - 1)
    return ret + jnp.where(n < max_exact, n, large)


def windowed_gqa(h, w_qkv, w_o, sink, rel_bias):
    B, L, _ = h.shape
    nb = L // BLOCK
    q, k, v = jnp.split(h @ w_qkv, [N_HEADS * HEAD_DIM, (N_HEADS + N_KV_HEADS) * HEAD_DIM], axis=-1)
    q = q.reshape(B, nb, BLOCK, N_KV_HEADS, GQA_GROUP, HEAD_DIM)

    def band(t):
        t = t.reshape(B, L, N_KV_HEADS, HEAD_DIM)
        t = jnp.pad(t, ((0, 0), (BLOCK, BLOCK), (0, 0), (0, 0))).reshape(B, nb + 2, BLOCK, N_KV_HEADS, HEAD_DIM)
        return jnp.concatenate([t[:, :-2], t[:, 1:-1], t[:, 2:]], axis=2)

    kb, vb = band(k), band(v)
    s = jnp.einsum('bnqkgd,bnpkd->bnkgqp', q, kb).astype(jnp.float32) / math.sqrt(HEAD_DIM)
    qi = jnp.arange(BLOCK)[:, None]
    pj = jnp.arange(3 * BLOCK)[None, :] - BLOCK
    rel = pj - qi
    bias = rel_bias.astype(jnp.float32)[t5_buckets(rel)]
    bias = bias.transpose(2, 0, 1).reshape(N_KV_HEADS, GQA_GROUP, BLOCK, 3 * BLOCK)
    kpos = jnp.arange(nb)[:, None] * BLOCK + pj
    valid = (jnp.abs(rel) <= WINDOW)[None] & ((kpos >= 0) & (kpos < L))[:, None, :]
    s = jnp.where(valid[None, :, None, None], s + bias, -jnp.inf)
    sink_l = sink.astype(jnp.float32).reshape(1, 1, N_KV_HEADS, GQA_GROUP, 1, 1)
    m = jnp.maximum(jnp.max(s, axis=-1, keepdims=True), sink_l)
    p = jnp.exp(s - m)
    p = p / (jnp.sum(p, axis=-1, keepdims=True) + jnp.exp(sink_l - m))
    o = jnp.einsum('bnkgqp,bnpkd->bnqkgd', p.astype(h.dtype), vb).reshape(B, L, N_HEADS * HEAD_DIM)
    return o @ w_o


def trunk(x, p):
    for layer in range(DEPTH):
        x = x + 0.5 * swiglu(rmsnorm(x, p['norm_ffn1'][layer]), p['ffn1_wi'][layer], p['ffn1_wo'][layer])
        h = rmsnorm(x, p['norm_mix'][layer])
        i = layer // 2
        if layer % 2 == 0:
            u = h @ p['ab_w_in'][i]
            ua, ub = u[..., :POOL_WIDTH], u[..., POOL_WIDTH:]
            ya = multiscale_pool(ua, p['pool_w'][i], p['pool_scale'][i])
            yb = hyena(ub, p['hy_conv_w'][i], p['hy_conv_b'][i],
                       p['hy_ff_w1'][i], p['hy_ff_b1'][i], p['hy_ff_w2'][i], p['hy_ff_b2'][i],
                       p['hy_ff_w3'][i], p['hy_ff_b3'][i], p['hy_ff_w4'][i], p['hy_ff_b4'][i],
                       p['hy_freq'][i], p['hy_d'][i])
            x = x + jnp.concatenate([ya, yb], axis=-1) @ p['ab_w_out'][i]
        else:
            x = x + windowed_gqa(h, p['attn_w_qkv'][i], p['attn_w_o'][i], p['attn_sink'][i], p['rel_bias'])
        x = x + 0.5 * swiglu(rmsnorm(x, p['norm_ffn2'][layer]), p['ffn2_wi'][layer], p['ffn2_wo'][layer])
    return rmsnorm(x, p['norm_final'])


def setup_inputs(seed: int = 0) -> dict:
    key = jax.random.key(seed)
    ks = jax.random.split(key, 32)
    f32 = jnp.float32

    def nrm(k, shape, scale):
        return jax.random.normal(k, shape, f32) * scale

    def gain(k, shape):
        return 1.0 + 0.05 * jax.random.normal(k, shape, f32)

    return {
        'x_prompt': nrm(ks[0], (BATCH, SEQ, D_MODEL), 1.0),
        'x_sample': nrm(ks[1], (DEC_BATCH, DEC_SEQ, D_MODEL), 1.0),
        'norm_ffn1': gain(ks[2], (DEPTH, D_MODEL)),
        'ffn1_wi': nrm(ks[3], (DEPTH, D_MODEL, 2 * D_FF), D_MODEL ** -0.5),
        'ffn1_wo': nrm(ks[4], (DEPTH, D_FF, D_MODEL), D_FF ** -0.5),
        'norm_mix': gain(ks[5], (DEPTH, D_MODEL)),
        'ab_w_in': nrm(ks[6], (N_EVEN, D_MODEL, AB_IN), D_MODEL ** -0.5),
        'pool_w': nrm(ks[7], (N_EVEN, N_POOL_GROUPS, POOL_GROUP, POOL_GROUP), POOL_GROUP ** -0.5),
        'pool_scale': 1.0 + 0.1 * jax.random.normal(ks[8], (N_EVEN, POOL_WIDTH), f32),
        'hy_conv_w': nrm(ks[9], (N_EVEN, 3, 3 * HYENA_WIDTH), 3 ** -0.5),
        'hy_conv_b': nrm(ks[10], (N_EVEN, 3 * HYENA_WIDTH), 0.02),
        'hy_ff_w1': nrm(ks[11], (N_EVEN, HYENA_EMB_DIM, HYENA_FILTER_ORDER), HYENA_EMB_DIM ** -0.5),
        'hy_ff_b1': nrm(ks[12], (N_EVEN, HYENA_FILTER_ORDER), 0.1),
        'hy_ff_w2': nrm(ks[13], (N_EVEN, HYENA_FILTER_ORDER, HYENA_FILTER_ORDER), HYENA_FILTER_ORDER ** -0.5),
        'hy_ff_b2': nrm(ks[14], (N_EVEN, HYENA_FILTER_ORDER), 0.1),
        'hy_ff_w3': nrm(ks[15], (N_EVEN, HYENA_FILTER_ORDER, HYENA_FILTER_ORDER), HYENA_FILTER_ORDER ** -0.5),
        'hy_ff_b3': nrm(ks[16], (N_EVEN, HYENA_FILTER_ORDER), 0.1),
        'hy_ff_w4': nrm(ks[17], (N_EVEN, HYENA_FILTER_ORDER, 2 * HYENA_WIDTH), HYENA_FILTER_ORDER ** -0.5),
        'hy_ff_b4': nrm(ks[18], (N_EVEN, 2 * HYENA_WIDTH), 0.1),
        'hy_freq': 1.0 + 0.1 * jax.random.normal(ks[19], (N_EVEN, HYENA_FILTER_ORDER), f32),
        'hy_d': nrm(ks[20], (N_EVEN, HYENA_WIDTH), 1.0),
        'ab_w_out': nrm(ks[21], (N_EVEN, D_MODEL, D_MODEL), D_MODEL ** -0.5),
        'attn_w_qkv': nrm(ks[22], (N_ODD, D_MODEL, QKV_OUT), D_MODEL ** -0.5),
        'attn_w_o': nrm(ks[23], (N_ODD, N_HEADS * HEAD_DIM, D_MODEL), (N_HEADS * HEAD_DIM) ** -0.5),
        'attn_sink': nrm(ks[24], (N_ODD, N_HEADS), 1.0),
        'rel_bias': nrm(ks[25], (N_BUCKETS, N_HEADS), 0.5),
        'norm_ffn2': gain(ks[26], (DEPTH, D_MODEL)),
        'ffn2_wi': nrm(ks[27], (DEPTH, D_MODEL, 2 * D_FF), D_MODEL ** -0.5),
        'ffn2_wo': nrm(ks[28], (DEPTH, D_FF, D_MODEL), D_FF ** -0.5),
        'norm_final': gain(ks[29], (D_MODEL,)),
    }


def reference(x_prompt, x_sample, norm_ffn1, ffn1_wi, ffn1_wo, norm_mix, ab_w_in, pool_w, pool_scale,
              hy_conv_w, hy_conv_b, hy_ff_w1, hy_ff_b1, hy_ff_w2, hy_ff_b2, hy_ff_w3, hy_ff_b3,
              hy_ff_w4, hy_ff_b4, hy_freq, hy_d, ab_w_out, attn_w_qkv, attn_w_o, attn_sink, rel_bias,
              norm_ffn2, ffn2_wi, ffn2_wo, norm_final):
    p = dict(norm_ffn1=norm_ffn1, ffn1_wi=ffn1_wi, ffn1_wo=ffn1_wo, norm_mix=norm_mix,
             ab_w_in=ab_w_in, pool_w=pool_w, pool_scale=pool_scale,
             hy_conv_w=hy_conv_w, hy_conv_b=hy_conv_b, hy_ff_w1=hy_ff_w1, hy_ff_b1=hy_ff_b1,
             hy_ff_w2=hy_ff_w2, hy_ff_b2=hy_ff_b2, hy_ff_w3=hy_ff_w3, hy_ff_b3=hy_ff_b3,
             hy_ff_w4=hy_ff_w4, hy_ff_b4=hy_ff_b4, hy_freq=hy_freq, hy_d=hy_d, ab_w_out=ab_w_out,
             attn_w_qkv=attn_w_qkv, attn_w_o=attn_w_o, attn_sink=attn_sink, rel_bias=rel_bias,
             norm_ffn2=norm_ffn2, ffn2_wi=ffn2_wi, ffn2_wo=ffn2_wo, norm_final=norm_final)
    y_prompt = trunk(x_prompt, p)
    y_sample = trunk(x_sample, p)
    return (y_prompt, y_sample)
```

```python
import math
from contextlib import ExitStack

import numpy as np
import ml_dtypes

import concourse.bass as bass
import concourse.mybir as mybir
from concourse.bass_utils import run_bass_kernel_spmd

F32 = mybir.dt.float32
BF16 = mybir.dt.bfloat16
AF = mybir.ActivationFunctionType
ALU = mybir.AluOpType
AX = mybir.AxisListType

D = 2048
DC = 16
DFF = 5632
FC = 44
NH = 16
NKV = 4
HD = 128
EPS = 1e-6
POOL_WINDOWS = (2, 4, 8, 16)
CH = 1024
NEG = -30000.0

CFG = dict(Lp=8192, Ls=2048, depth=4, T=1024, plan=None)


def _esize(dt):
    return 4 if dt == F32 else 2


class Sched:
    SEM_LIMIT = 24000

    def __init__(self, nc, stack):
        self.nc = nc
        self.stack = stack
        self.eng = {"pe": nc.tensor, "act": nc.scalar, "dve": nc.vector, "pool": nc.gpsimd, "sp": nc.sync}
        self.nsem = 0
        self.csem = {}
        self.observed = {e: {} for e in self.eng}
        self.records = {}
        self.dma_slots = {}
        self.dma_rr = {}
        for e in ("sp", "pool", "act"):
            self.dma_slots[e] = [[self.new_sem(), 0] for _ in range(8)]
            self.dma_rr[e] = 0
        self.nops = 0

    def new_sem(self):
        self.nsem += 1
        return self.stack.enter_context(self.nc.semaphore(f"s{self.nsem}"))

    @staticmethod
    def rng(ap):
        t = ap.tensor
        es = _esize(ap.dtype)
        pairs = ap.ap
        off = ap.offset
        kind = type(t).__name__
        if "DRam" in kind:
            lo = off
            hi = off
            for st, cnt in pairs:
                if st >= 0:
                    hi += st * (cnt - 1)
                else:
                    lo += st * (cnt - 1)
            return (t.name, lo * es, (hi + 1) * es)
        rowlen = pairs[0][0]
        f = off % rowlen if rowlen > 0 else off
        lo = f
        hi = f
        for st, cnt in pairs[1:]:
            if st >= 0:
                hi += st * (cnt - 1)
            else:
                lo += st * (cnt - 1)
        return (t.name, lo * es, (hi + 1) * es)

    def _wait(self, e, sem, val):
        key = id(sem)
        ob = self.observed[e]
        if ob.get(key, 0) >= val:
            return
        self.eng[e].wait_ge(sem, val)
        ob[key] = val

    def op(self, e, fn, reads=(), writes=(), dma=False, sig=True):
        self.nops += 1
        rr = [self.rng(a) for a in reads]
        ww = [self.rng(a) for a in writes]
        deps = {}
        for sp, lo, hi in rr:
            for rec in self.records.get(sp, ()):
                if rec[4] and rec[0] < hi and lo < rec[1]:
                    k = id(rec[2])
                    if deps.get(k, (None, 0))[1] < rec[3]:
                        deps[k] = (rec[2], rec[3])
        for sp, lo, hi in ww:
            for rec in self.records.get(sp, ()):
                if rec[0] < hi and lo < rec[1]:
                    k = id(rec[2])
                    if deps.get(k, (None, 0))[1] < rec[3]:
                        deps[k] = (rec[2], rec[3])
        if dma:
            slots = self.dma_slots[e]
            si = self.dma_rr[e]
            self.dma_rr[e] = (si + 1) % len(slots)
            slot = slots[si]
            if slot[1] >= self.SEM_LIMIT:
                slot[0] = self.new_sem()
                slot[1] = 0
            elif slot[1] > 0:
                k = id(slot[0])
                if deps.get(k, (None, 0))[1] < slot[1]:
                    deps[k] = (slot[0], slot[1])
        for sem, val in deps.values():
            self._wait(e, sem, val)
        ins = fn()
        if dma:
            slot[1] += 16
            ins.then_inc(slot[0], 16)
            ev = (slot[0], slot[1])
        else:
            cs = self.csem.get(e)
            if cs is None or cs[1] >= self.SEM_LIMIT:
                cs = [self.new_sem(), 0]
                self.csem[e] = cs
            cs[1] += 1
            ins.then_inc(cs[0], 1)
            ev = (cs[0], cs[1])
        for sp, lo, hi in ww:
            lst = self.records.setdefault(sp, [])
            lst[:] = [r for r in lst if not (lo <= r[0] and r[1] <= hi)]
            lst.append([lo, hi, ev[0], ev[1], True])
        for sp, lo, hi in rr:
            lst = self.records.setdefault(sp, [])
            lst[:] = [r for r in lst if not ((not r[4]) and r[2] is ev[0] and lo <= r[0] and r[1] <= hi)]
            lst.append([lo, hi, ev[0], ev[1], False])
        return ins

    def mm(self, out, pairs):
        n = len(pairs)
        reads = [a for p in pairs for a in p]

        def fn():
            ins = None
            for i, (l, r) in enumerate(pairs):
                ins = self.nc.tensor.matmul(out, l, r, start=(i == 0), stop=(i == n - 1))
            return ins

        return self.op("pe", fn, reads=reads, writes=[out])

    def tr(self, items, ident):
        reads = [i for _, i in items] + [ident]
        writes = [o for o, _ in items]

        def fn():
            ins = None
            for o, i in items:
                k = i.shape[0]
                ins = self.nc.tensor.transpose(o, i, ident[0:k, 0:k])
            return ins

        return self.op("pe", fn, reads=reads, writes=writes)

    def dma(self, e, out, in_):
        return self.op(e, lambda: self.eng[e].dma_start(out=out, in_=in_), reads=[in_], writes=[out], dma=True)

    def act(self, out, in_, func, bias=None, scale=None, accum_out=None, e="act"):
        reads = [in_]
        kw = {}
        if bias is not None:
            kw["bias"] = bias
            if not isinstance(bias, (int, float)):
                reads.append(bias)
        if scale is not None:
            kw["scale"] = scale
            if not isinstance(scale, (int, float)):
                reads.append(scale)
        writes = [out]
        if accum_out is not None:
            kw["accum_out"] = accum_out
            writes.append(accum_out)
        return self.op("act", lambda: self.nc.scalar.activation(out, in_, func, **kw), reads=reads, writes=writes)

    def tt(self, out, in0, in1, op, e="dve"):
        return self.op(e, lambda: self.eng[e].tensor_tensor(out, in0, in1, op), reads=[in0, in1], writes=[out])

    def ts(self, out, in0, s1, s2, op0, op1=None, e="dve"):
        reads = [in0] + [s for s in (s1, s2) if s is not None and not isinstance(s, (int, float))]
        if op1 is None:
            return self.op(e, lambda: self.eng[e].tensor_scalar(out, in0, s1, None, op0), reads=reads, writes=[out])
        return self.op(e, lambda: self.eng[e].tensor_scalar(out, in0, s1, s2, op0, op1), reads=reads, writes=[out])

    def stt(self, out, in0, scalar, in1, op0, op1, e="dve"):
        reads = [in0, in1] + ([] if isinstance(scalar, (int, float)) else [scalar])
        return self.op(e, lambda: self.eng[e].scalar_tensor_tensor(out, in0, scalar, in1, op0, op1),
                       reads=reads, writes=[out])

    def copy(self, out, in_, e="dve"):
        if e == "act":
            return self.op("act", lambda: self.nc.scalar.copy(out, in_), reads=[in_], writes=[out])
        return self.op(e, lambda: self.eng[e].tensor_copy(out, in_), reads=[in_], writes=[out])

    def memset(self, out, val, e="dve"):
        return self.op(e, lambda: self.eng[e].memset(out, val), reads=[], writes=[out])

    def reduce(self, out, in_, op, absval=False, e="dve"):
        kw = {"apply_absolute_value": True} if absval else {}
        return self.op(e, lambda: self.eng[e].tensor_reduce(out, in_, AX.X, op, **kw), reads=[in_], writes=[out])

    def recip(self, out, in_):
        return self.op("dve", lambda: self.nc.vector.reciprocal(out, in_), reads=[in_], writes=[out])

    def drain(self):
        for e, cs in self.csem.items():
            self._wait("sp", cs[0], cs[1])
        for e, slots in self.dma_slots.items():
            for sem, val in slots:
                if val > 0:
                    self._wait("sp", sem, val)


def sv(ap, off, dims):
    return bass.AP(tensor=ap.tensor, offset=ap.offset + off, ap=[list(ap.ap[0])] + [list(d) for d in dims])


def fft_tables(L):
    N = 2 * L
    N2 = N // 128
    G = 128 // N2
    n2 = np.arange(N2)
    k1 = np.arange(128)
    ang = 2 * np.pi * np.outer(n2, k1) / N
    c = np.tile(np.cos(ang), (G, 1))
    d = np.tile(-np.sin(ang), (G, 1))
    tw1 = np.stack([c, d, -d], axis=1).astype(np.float32)
    ang2 = 2 * np.pi * np.outer(k1, n2) / N
    c2 = np.tile(np.cos(ang2), (1, G)) / N
    d2 = np.tile(np.sin(ang2), (1, G)) / N
    tw2 = np.stack([c2, d2, -d2], axis=1).astype(np.float32)
    a = 2 * np.pi * np.outer(n2, n2) / N2
    bre = np.kron(np.eye(G), np.cos(a))
    bim = np.kron(np.eye(G), -np.sin(a))
    bd = np.stack([bre, bim, -bim], axis=1).astype(ml_dtypes.bfloat16)
    cre = np.kron(np.eye(G), np.cos(a))
    cim = np.kron(np.eye(G), np.sin(a))
    cinv = np.stack([np.concatenate([cre, cim], 1), np.concatenate([-cim, cre], 1)], axis=1)
    cinv = cinv.astype(ml_dtypes.bfloat16)
    f32 = np.float32
    t = np.linspace(0.0, 1.0, L, dtype=f32)
    w = (2.0 * math.pi * np.arange(L, dtype=f32) / L).astype(f32)
    f = np.linspace(1e-4, 15, 16, dtype=f32)
    fw = (f[None, :] * w[:, None]).astype(f32)
    z = np.concatenate([t[:, None], np.cos(fw), -np.sin(fw)], axis=-1).astype(f32)
    pos = np.concatenate([np.arange(L), [0], L - np.arange(1, L)])
    zext = np.ascontiguousarray(z[pos].T).astype(f32)
    text = t[pos].astype(f32)
    return dict(N2=N2, G=G, tw1=tw1, tw2=tw2, bd=bd, cinv=cinv, zext=zext, text=text.reshape(1, -1))


def common_tables():
    n1 = np.arange(128)
    a = 2 * np.pi * np.outer(n1, n1) / 128
    f1 = np.concatenate([np.cos(a), -np.sin(a)], axis=1).astype(ml_dtypes.bfloat16)
    e = np.stack([np.cos(a)[:, :64], -np.sin(a)[:, :64]], axis=1).astype(ml_dtypes.bfloat16)
    ident = np.eye(128, dtype=np.float32)
    jrev = np.ascontiguousarray(np.eye(128, dtype=np.float32)[::-1])
    rel = np.arange(-255, 257)
    nb = 16
    me = 8
    n = np.abs(rel)
    large = me + (np.log(np.maximum(n, 1).astype(np.float32) / me) / math.log(128 / me) * (nb - me)).astype(np.int32)
    large = np.minimum(large, nb - 1)
    bucket = (rel > 0).astype(np.int32) * nb + np.where(n < me, n, large)
    valid = n <= 128
    oh = np.zeros((32, 512), np.float32)
    oh[bucket[valid], np.nonzero(valid)[0]] = 1.0
    mask = np.where(valid, 0.0, NEG).astype(np.float32).reshape(1, 512)
    return dict(f1=f1, e=e, ident=ident, identb=ident.astype(ml_dtypes.bfloat16), jrev=jrev, oh=oh, mask=mask)


def pool_edges(L):
    ec = np.zeros((4, 16), np.float32)
    for g, w in enumerate(POOL_WINDOWS):
        t = np.concatenate([np.arange(8), np.arange(L - 8, L)])
        lo = np.clip(t - w // 2, 0, L)
        hi = np.clip(t + w // 2, 0, L)
        ec[g] = 1.0 / (hi - lo)
    return np.ascontiguousarray(np.broadcast_to(ec.reshape(1, 64), (128, 64)))


def fm(v, nchunk):
    return np.ascontiguousarray(np.asarray(v, np.float32).reshape(nchunk, 128).T)


class Arena:
    def __init__(self, ap, nbytes):
        self.ap = ap
        self.nbytes = nbytes
        self.top = 0

    def alloc(self, dt, *shape):
        n = 1
        for s in shape:
            n *= s
        nb = n * _esize(dt)
        nb = (nb + 63) // 64 * 64
        off = self.top
        self.top += nb
        assert self.top <= self.nbytes, f"SBUF arena overflow {self.top} > {self.nbytes}"
        v = self.ap[:, off // 4:(off + nb) // 4]
        if dt != F32:
            v = v.bitcast(dt)
        v = v[:, 0:n]
        if len(shape) == 2:
            v = v.rearrange("p (a b) -> p a b", a=shape[0])
        elif len(shape) == 3:
            v = v.rearrange("p (a b c) -> p a b c", a=shape[0], b=shape[1])
        return v


class Prog:
    def __init__(self, cfg, cols, ncol):
        self.cfg = cfg
        self.cols = cols
        self.ncol = ncol
        self.Lp, self.Ls, self.depth, self.T = cfg["Lp"], cfg["Ls"], cfg["depth"], cfg["T"]
        self.n_even = (self.depth + 1) // 2
        self.n_odd = self.depth // 2

    def declare(self, nc, host):
        self.nc = nc
        self.I = {}
        for name, arr in host.items():
            dt = F32 if arr.dtype == np.float32 else BF16
            self.I[name] = nc.dram_tensor(name, list(arr.shape), dt, kind="ExternalInput").ap()
        Lp, Ls = self.Lp, self.Ls
        Lm = max(Lp, Ls)
        self.O = {
            "yp": nc.dram_tensor("yp", [Lp, D], F32, kind="ExternalOutput").ap(),
            "ys": nc.dram_tensor("ys", [Ls, D], F32, kind="ExternalOutput").ap(),
        }

        def scr(name, shape, dt):
            return nc.dram_tensor(name, shape, dt, kind="Internal").ap()

        self.seqs = [
            dict(name="p", x=self.I["xp"], y=self.O["yp"], L=Lp, XS=scr("XSp", [128, DC * Lp], F32)),
            dict(name="s", x=self.I["xs"], y=self.O["ys"], L=Ls, XS=scr("XSs", [128, DC * Ls], F32)),
        ]
        self.U = scr("U", [128, 32 * Lm], F32)
        self.PB = scr("PB", [128, 8 * Lm], BF16)
        self.YB = scr("YB", [128, 8 * Lm], BF16)
        self.KF = scr("KF", [128, (Lm // 64) * 256], F32)
        self.QT = scr("QT", [128, 16 * Lm], BF16)
        self.KT = scr("KT", [128, 4 * Lm], BF16)
        self.VT = scr("VT", [128, (Lm // 128) * 512], BF16)
        self.OT = scr("OT", [128, 16 * Lm], BF16)
        self.TB = scr("TB", [16, 512], F32)
        self.WBI = scr("WBI", [128, (FC // 2) * DC * 512], BF16)
        self.WBO = scr("WBO", [128, 8 * FC * 256], BF16)
        self.WBM = scr("WBM", [128, 16 * DC * 256], BF16)
        self.WBM2 = scr("WBM2", [128, 8 * DC * 256], BF16)
        self.WBP = scr("WBP", [128, 8 * 256], BF16)
        self.H3 = scr("H3", [64, 2 * Lm], F32)
        self.BIAS = scr("BIAS", [128, 16 * 384], F32)

    def build(self, nc, host):
        self.declare(nc, host)
        with ExitStack() as stack:
            SBW = 45056
            arena_t = stack.enter_context(nc.sbuf_tensor("arena", [128, SBW], F32))
            const_t = stack.enter_context(nc.sbuf_tensor("consts", [128, 3072], F32))
            ps_t = stack.enter_context(nc.psum_tensor("ps", [128, 4096], F32))
            self.S = Sched(nc, stack)
            self.A = Arena(arena_t[:, :], SBW * 4)
            self.CA = Arena(const_t[:, :], 3072 * 4)
            self.ps = ps_t[:, :]
            self.psb = ps_t[:, :].bitcast(BF16)
            self.bankrr = 0
            self.load_consts()
            for seq in self.seqs:
                self.convert_in(seq)
            plan = self.cfg.get("plan")
            if plan is None:
                plan = []
                for l in range(self.depth):
                    plan += [("ffn", 1, l), ("mix", l), ("ffn", 2, l)]
            for st in plan:
                if st[0] == "ffn":
                    self.ffn(st[1], st[2])
                elif st[1] % 2 == 0:
                    self.mix_even(st[1] // 2, st[1])
                else:
                    self.mix_odd(st[1] // 2, st[1])
            for seq in self.seqs:
                self.final_out(seq)
            self.S.drain()
        return nc

    def bank(self, i=None):
        if i is None:
            i = self.bankrr
            self.bankrr = (self.bankrr + 1) % 8
        return self.ps[:, 512 * i:512 * (i + 1)]

    def bankb(self, i):
        return self.psb[:, 1024 * i:1024 * (i + 1)]

    def col(self, name, n=None, i=0):
        c0, w = self.cols[name]
        if n is None:
            return self.cv[:, c0:c0 + w]
        return self.cv[:, c0 + i:c0 + i + n]

    def load_consts(self):
        S, CA, I = self.S, self.CA, self.I
        self.cv = CA.alloc(F32, self.ncol)
        S.dma("sp", self.cv, I["cvec"])
        self.ident = CA.alloc(F32, 128)
        S.dma("sp", self.ident, I["ident"])
        self.identb = CA.alloc(BF16, 128)
        S.dma("sp", self.identb, I["identb"])
        self.ones = CA.alloc(BF16, 128)
        S.memset(self.ones, 1.0)
        self.f1 = CA.alloc(BF16, 256)
        S.dma("sp", self.f1, I["f1"])
        self.etab = CA.alloc(BF16, 2, 64)
        S.dma("sp", self.etab, I["etab"])
        g0, gw = self.cols["gains"]
        S.ts(self.cv[:, g0:g0 + gw], self.cv[:, g0:g0 + gw], math.sqrt(D), 0.0, ALU.mult, ALU.add)
        self.epsc = CA.alloc(F32, 4)
        S.memset(self.epsc, float(D * EPS))
        for i in range(self.n_even):
            for k in range(3):
                c0 = self.cols["mlp"][0] + 8 * i
                S.ts(self.cv[0:64, c0 + 4 + k:c0 + 5 + k], self.cv[0:64, c0 + k:c0 + k + 1],
                     self.cv[0:64, c0 + 3:c0 + 4], 0.0, ALU.mult, ALU.add)

    def convert_in(self, seq):
        S, A = self.S, self.A
        L = seq["L"]
        XS = seq["XS"].rearrange("p (c t) -> p c t", c=DC)
        A.top = 0
        xin = A.alloc(F32, 4, D)
        xo = A.alloc(F32, DC, 512)
        for t0 in range(0, L, 512):
            for k in range(4):
                S.dma("sp", xin[:, k, :], seq["x"][t0 + 128 * k:t0 + 128 * (k + 1), :])
            for c in range(DC):
                b = self.bank()
                S.tr([(b[:, 128 * k:128 * (k + 1)], xin[:, k, 128 * c:128 * (c + 1)]) for k in range(4)], self.ident)
                S.copy(xo[:, c, :], b, e=("act" if c % 2 else "dve"))
            S.dma("sp", XS[:, :, t0:t0 + 512], xo)

    def norm_tile(self, XSv, t0, gcol, xin, sqb, rt, out_fn):
        S = self.S
        S.dma("sp", xin, XSv[:, :, t0:t0 + 512])
        S.act(sqb, xin, AF.Square)
        b = self.bank()
        S.mm(b, [(self.ones, sqb[:, c, :]) for c in range(DC)])
        S.act(rt, b, AF.Ln, bias=self.epsc[:, 0:1])
        S.act(rt, rt, AF.Exp, scale=-0.5)
        for c in range(DC):
            out_fn(c, xin[:, c, :], rt, gcol[:, c:c + 1])

    def final_out(self, seq):
        S, A = self.S, self.A
        L = seq["L"]
        XS = seq["XS"].rearrange("p (c t) -> p c t", c=DC)
        gcol = self.col("gains", 16, 16 * 3 * self.depth)
        A.top = 0
        xin = A.alloc(F32, DC, 512)
        sqb = A.alloc(BF16, DC, 512)
        rt = A.alloc(F32, 512)
        hf = A.alloc(F32, DC, 512)
        yo = A.alloc(F32, 4, D)
        for t0 in range(0, L, 512):
            self.norm_tile(XS, t0, gcol, xin, sqb, rt,
                           lambda c, xc, r, g: S.stt(hf[:, c, :], xc, g, r, ALU.mult, ALU.mult))
            for k in range(4):
                for cq in range(4):
                    b = self.bank()
                    S.tr([(b[:, 128 * j:128 * (j + 1)], hf[:, 4 * cq + j, 128 * k:128 * (k + 1)]) for j in range(4)],
                         self.ident)
                    S.copy(yo[:, k, 512 * cq:512 * (cq + 1)], b, e=("act" if cq % 2 else "dve"))
            for k in range(4):
                S.dma("sp", seq["y"][t0 + 128 * k:t0 + 128 * (k + 1), :], yo[:, k, :])

    def prep(self, W, nk, col_blocks, dst, bw):
        S, A = self.S, self.A
        A.top = 0
        KH = 16 if nk <= 16 else (nk + 1) // 2
        stg = [A.alloc(F32, KH, 256) for _ in range(2)]
        ob = [A.alloc(BF16, nk, bw) for _ in range(2)]
        ncw = W.shape[1]
        n = 0
        for b, runs in enumerate(col_blocks):
            o = ob[b % 2]
            j0 = 0
            for (c0, w) in runs:
                for k0 in range(0, nk, KH):
                    kn = min(KH, nk - k0)
                    st = stg[n % 2]
                    src = bass.AP(tensor=W.tensor, offset=W.offset + (k0 * 128) * ncw + c0,
                                  ap=[[ncw, 128], [128 * ncw, kn], [1, w]])
                    S.dma("sp", st[:, 0:kn, 0:w], src)
                    S.copy(o[:, k0:k0 + kn, j0:j0 + w], st[:, 0:kn, 0:w], e=("act" if n % 2 else "dve"))
                    n += 1
                j0 += w
            S.dma("sp", sv(dst, b * 128 * nk * bw, [[nk * bw, 128], [1, nk * bw]]) if False else
                  bass.AP(tensor=dst.tensor, offset=dst.offset + b * 128 * nk * bw, ap=[[nk * bw, 128], [1, nk * bw]]),
                  o.rearrange("p k j -> p (k j)"))

    def wget(self, slot, scr, b, nk, bw):
        src = bass.AP(tensor=scr.tensor, offset=scr.offset + b * 128 * nk * bw, ap=[[nk * bw, 128], [1, nk * bw]])
        self.S.dma("sp", slot.rearrange("p k j -> p (k j)"), src)

    def group_norm(self, seq, t0, T, gcol, h, scratch):
        S = self.S
        XS = seq["XS"].rearrange("p (c t) -> p c t", c=DC)
        xin, sqb, rt = scratch
        for s in range(T // 512):
            self.norm_tile(XS, t0 + 512 * s, gcol, xin, sqb, rt,
                           lambda c, xc, r, g, s=s: S.stt(h[:, c, 512 * s:512 * (s + 1)], xc, g, r, ALU.mult, ALU.mult))

    def out_proj(self, seq, t0, T, act, nk, W, scale, wslots, xsl, xo):
        S = self.S
        XS = seq["XS"].rearrange("p (c t) -> p c t", c=DC)
        for ob in range(8):
            wb = wslots[ob % 2]
            self.wget(wb, W, ob, nk, 256)
            xs_ = xsl[ob % 2]
            xo_ = xo[ob % 2]
            S.dma("sp", xs_, XS[:, 2 * ob:2 * ob + 2, t0:t0 + T])
            for oo in range(2):
                for s in range(T // 512):
                    b = self.bank()
                    S.mm(b, [(wb[:, kc, 128 * oo:128 * (oo + 1)], act[:, kc, 512 * s:512 * (s + 1)]) for kc in range(nk)])
                    S.stt(xo_[:, oo, 512 * s:512 * (s + 1)], b, float(scale), xs_[:, oo, 512 * s:512 * (s + 1)],
                          ALU.mult, ALU.add)
            S.dma("sp", XS[:, 2 * ob:2 * ob + 2, t0:t0 + T], xo_)

    def ffn(self, which, layer):
        S, A, T = self.S, self.A, self.T
        wi = self.I[f"ffn{which}_wi"][layer]
        wo = self.I[f"ffn{which}_wo"][layer]
        gcol = self.col("gains", 16, 16 * ((0 if which == 1 else 2) * self.depth + layer))
        self.prep(wi, DC, [[(256 * jb, 256), (DFF + 256 * jb, 256)] for jb in range(FC // 2)], self.WBI, 512)
        self.prep(wo, FC, [[(256 * ob, 256)] for ob in range(8)], self.WBO, 256)
        wo = self.WBO
        A.top = 0
        h = A.alloc(BF16, DC, T)
        act = A.alloc(BF16, FC, T)
        wtop = A.top
        wslots = [A.alloc(BF16, DC, 2, 256) for _ in range(3)]
        A.top = wtop
        woslots = [A.alloc(BF16, FC, 256) for _ in range(2)]
        A.top = max(A.top, wtop + 3 * 16384)
        sg = [A.alloc(F32, 512) for _ in range(2)]
        rt = A.alloc(F32, 512)
        top = A.top
        A.top = (DC * T * 2)
        xin = A.alloc(F32, DC, 512)
        sqb = A.alloc(BF16, DC, 512)
        A.top = 0
        xsl = [A.alloc(F32, 2, T) for _ in range(2)]
        xo = [A.alloc(F32, 2, T) for _ in range(2)]
        assert A.top <= DC * T * 2
        A.top = top
        for seq in self.seqs:
            for t0 in range(0, seq["L"], T):
                self.group_norm(seq, t0, T, gcol, h, (xin, sqb, rt))
                for jb in range(FC // 2):
                    wb = wslots[jb % 3]
                    self.wget(wb.rearrange("p k a j -> p k (a j)"), self.WBI, jb, DC, 512)
                    for s in range(T // 512):
                        for jj in range(2):
                            gb, ub = self.bank(), self.bank()
                            hs = [h[:, kc, 512 * s:512 * (s + 1)] for kc in range(DC)]
                            S.mm(gb, [(wb[:, kc, 0, 128 * jj:128 * (jj + 1)], hs[kc]) for kc in range(DC)])
                            S.mm(ub, [(wb[:, kc, 1, 128 * jj:128 * (jj + 1)], hs[kc]) for kc in range(DC)])
                            sgt = sg[jj]
                            S.act(sgt, gb, AF.Silu)
                            S.tt(act[:, 2 * jb + jj, 512 * s:512 * (s + 1)], sgt, ub, ALU.mult)
                self.out_proj(seq, t0, T, act, FC, wo, 0.5, woslots, xsl, xo)

    def build_bias(self):
        S, A, I = self.S, self.A, self.I
        A.top = 0
        rb = A.alloc(F32, 16)
        oh = A.alloc(F32, 512)
        mk = A.alloc(F32, 512)
        on1 = A.alloc(F32, 16)
        jr = A.alloc(F32, 128)
        tb = A.alloc(F32, 512)
        hk = A.alloc(F32, 384)
        bt = A.alloc(F32, 16, 384)
        S.dma("sp", rb[0:32, :], I["rel_bias"])
        S.dma("sp", oh[0:32, :], I["oh"])
        S.dma("sp", mk[0:1, :], I["mask"])
        S.dma("sp", jr, I["jrev"])
        S.memset(on1[0:1, :], 1.0)
        b = self.bank()
        S.mm(b[0:16, :], [(rb[0:32, :], oh[0:32, :]), (on1[0:1, :], mk[0:1, :])])
        S.copy(tb[0:16, :], b[0:16, :])
        S.dma("sp", self.TB, tb[0:16, :])
        for hh in range(16):
            src = bass.AP(tensor=self.TB.tensor, offset=self.TB.offset + 512 * hh, ap=[[1, 128], [1, 384]])
            S.dma("sp", hk, src)
            b = self.bank()
            S.mm(b[:, 0:384], [(jr, hk)])
            S.copy(bt[:, hh, :], b[:, 0:384])
        S.dma("sp", self.BIAS.rearrange("p (h k) -> p h k", h=16), bt)

    def mix_odd(self, i, layer):
        S, A, T = self.S, self.A, self.T
        if not getattr(self, "_bias_done", False):
            self.build_bias()
            self._bias_done = True
        wqkv = self.I["attn_w_qkv"][i]
        self.prep(wqkv, DC, [[(256 * b, 256)] for b in range(12)], self.WBM, 256)
        self.prep(self.I["attn_w_o"][i], DC, [[(256 * b, 256)] for b in range(8)], self.WBM2, 256)
        wo = self.WBM2
        gcol = self.col("gains", 16, 16 * (self.depth + layer))
        sink = self.col("sink", 16, 16 * i)
        for seq in self.seqs:
            L = seq["L"]
            nb = L // 128
            QT = self.QT.rearrange("p (c t) -> p c t", c=16)[:, :, 0:L] if False else sv(self.QT, 0, [[L, 16], [1, L]])
            KT = sv(self.KT, 0, [[L, 4], [1, L]])
            VT = sv(self.VT, 0, [[512, nb], [1, 512]])
            OT = sv(self.OT, 0, [[L, 16], [1, L]])
            A.top = 0
            h = A.alloc(BF16, DC, T)
            qk = A.alloc(BF16, 20, T)
            vst = A.alloc(BF16, T // 128, 512)
            wslots = [A.alloc(BF16, DC, 256) for _ in range(3)]
            wv = [A.alloc(BF16, DC, 256) for _ in range(2)]
            xin = A.alloc(F32, DC, 512)
            sqb = A.alloc(BF16, DC, 512)
            rt = A.alloc(F32, 512)
            for t0 in range(0, L, T):
                self.group_norm(seq, t0, T, gcol, h, (xin, sqb, rt))
                for cb in range(10):
                    wb = wslots[cb % 3]
                    self.wget(wb, self.WBM, cb, DC, 256)
                    for jj in range(2):
                        for s in range(T // 512):
                            b = self.bank()
                            S.mm(b, [(wb[:, kc, 128 * jj:128 * (jj + 1)], h[:, kc, 512 * s:512 * (s + 1)]) for kc in range(DC)])
                            S.copy(qk[:, 2 * cb + jj, 512 * s:512 * (s + 1)], b, e=("act" if s % 2 else "dve"))
                self.wget(wv[0], self.WBM, 10, DC, 256)
                self.wget(wv[1], self.WBM, 11, DC, 256)
                for tt in range(T // 128):
                    b = self.bank()
                    for hv in range(2):
                        S.mm(b[:, 256 * hv:256 * (hv + 1)], [(h[:, kc, 128 * tt:128 * (tt + 1)], wv[hv][:, kc, :]) for kc in range(DC)])
                    S.copy(vst[:, tt, :], b, e=("act" if tt % 2 else "dve"))
                S.dma("sp", QT[:, :, t0:t0 + T], qk[:, 0:16, :])
                S.dma("sp", KT[:, :, t0:t0 + T], qk[:, 16:20, :])
                S.dma("sp", VT[:, t0 // 128:(t0 + T) // 128, :], vst)
            A.top = 0
            bias = A.alloc(F32, 16, 384)
            S.dma("sp", bias, self.BIAS.rearrange("p (h k) -> p h k", h=16))
            ktg = A.alloc(BF16, L)
            vtg = A.alloc(BF16, nb, 128)
            QB = min(8, nb)
            qt = A.alloc(BF16, 4, 128 * QB)
            ost = A.alloc(BF16, 4, 128 * QB)
            sc = A.alloc(F32, 4, 384)
            pe_ = A.alloc(F32, 4, 384)
            pb = A.alloc(BF16, 4, 384)
            pT = A.alloc(BF16, 3, 512)
            st = A.alloc(F32, 32)
            isq = 1.0 / math.sqrt(HD)
            for g in range(NKV):
                S.dma("sp", ktg, KT[:, g, :])
                S.dma("sp", vtg, VT[:, :, 128 * g:128 * (g + 1)])
                sk4 = sink[:, 4 * g:4 * g + 4]
                for blk in range(nb):
                    bq = blk % QB
                    if bq == 0:
                        S.dma("sp", qt, QT[:, 4 * g:4 * g + 4, 128 * blk:128 * (blk + QB)])
                    kb0, kb1 = max(blk - 1, 0), min(blk + 1, nb - 1)
                    nkb = kb1 - kb0 + 1
                    W = 128 * nkb
                    boff = 128 * (kb0 - (blk - 1))
                    for h4 in range(4):
                        S.mm(self.bank(h4)[:, 0:W], [(qt[:, h4, 128 * bq:128 * (bq + 1)], ktg[:, 128 * kb0:128 * kb0 + W])])
                    psv = sv(self.ps, 0, [[512, 4], [1, W]])
                    S.stt(sc[:, :, 0:W], psv, isq, bias[:, 4 * g:4 * g + 4, boff:boff + W], ALU.mult, ALU.add)
                    rmax, negm, rsum, es, den, rinv, tmp = (st[:, 4 * k:4 * k + 4] for k in range(7))
                    S.reduce(rmax, sc[:, :, 0:W], ALU.max)
                    S.tt(negm, rmax, sk4, ALU.max)
                    S.ts(negm, negm, -1.0, None, ALU.mult)
                    S.memset(rsum, 0.0)
                    for h4 in range(4):
                        S.act(pe_[:, h4, 0:W], sc[:, h4, 0:W], AF.Exp, bias=negm[:, h4:h4 + 1],
                              accum_out=rsum[:, h4:h4 + 1])
                    S.tt(tmp, negm, sk4, ALU.add)
                    S.act(es, tmp, AF.Exp)
                    S.tt(den, rsum, es, ALU.add)
                    S.recip(rinv, den)
                    S.tt(pb[:, :, 0:W], pe_[:, :, 0:W], sv(rinv, 0, [[1, 4], [0, W]]), ALU.mult)
                    for kb in range(nkb):
                        bb = self.bankb(4 + kb)
                        S.tr([(bb[:, 128 * h4:128 * (h4 + 1)], pb[:, h4, 128 * kb:128 * (kb + 1)]) for h4 in range(4)],
                             self.identb)
                        S.copy(pT[:, kb, :], bb[:, 0:512], e=("act" if kb % 2 else "dve"))
                    ob_ = self.bank(7)
                    S.mm(ob_, [(vtg[:, kb0 + kb, :], pT[:, kb, :]) for kb in range(nkb)])
                    S.copy(sv(ost, 128 * bq, [[128 * QB, 4], [1, 128]]), sv(ob_, 0, [[128, 4], [1, 128]]), e="act")
                    if bq == QB - 1:
                        S.dma("sp", OT[:, 4 * g:4 * g + 4, 128 * (blk - QB + 1):128 * (blk + 1)], ost)
            A.top = 0
            o = A.alloc(BF16, DC, T)
            woslots = [A.alloc(BF16, DC, 256) for _ in range(2)]
            xsl = [A.alloc(F32, 2, T) for _ in range(2)]
            xo = [A.alloc(F32, 2, T) for _ in range(2)]
            for t0 in range(0, L, T):
                S.dma("sp", o, OT[:, :, t0:t0 + T])
                self.out_proj(seq, t0, T, o, DC, wo, 1.0, woslots, xsl, xo)

    def cmul(self, src_bank, tab, dst, k, tP, tQ):
        S = self.S
        a_all = sv(src_bank, 0, [[256, 2], [128, 2], [1, 128]])
        a_re = sv(src_bank, 0, [[256, 2], [1, 128]])
        a_im = sv(src_bank, 128, [[256, 2], [1, 128]])
        S.tt(sv(tP, 0, [[256, 2], [128, 2], [1, 128]]), a_all, sv(tab, 0, [[0, 2], [0, 2], [1, 128]]), ALU.mult)
        S.tt(sv(tQ, 0, [[256, 2], [1, 128]]), a_im, sv(tab, 256, [[0, 2], [1, 128]]), ALU.mult)
        S.tt(sv(tQ, 128, [[256, 2], [1, 128]]), a_re, sv(tab, 128, [[0, 2], [1, 128]]), ALU.mult)
        S.tt(sv(dst, 2 * k * 128, [[128, 2], [512, 2], [1, 128]]), sv(tP, 0, [[256, 2], [128, 2], [1, 128]]),
             sv(tQ, 0, [[256, 2], [128, 2], [1, 128]]), ALU.add)

    def fft_fwd(self, srcT, nK, N2, tb, bufs, sink):
        S = self.S
        Bt, tP, tQ = bufs
        for b in range(N2 // 4):
            for k in range(2):
                bk = self.bank()
                for g2 in range(2):
                    gi = 4 * b + 2 * k + g2
                    S.mm(bk[:, 256 * g2:256 * (g2 + 1)], [(srcT[0:nK, 128 * gi:128 * (gi + 1)], self.f1[0:nK, :])])
                self.cmul(bk, tb["tw1"], Bt, k, tP, tQ)
            bre = Bt[:, 0, :, :].rearrange("p g k -> p (g k)")
            bim = Bt[:, 1, :, :].rearrange("p g k -> p (g k)")
            xre, xim = self.bank(), self.bank()
            S.mm(xre, [(tb["bd"][:, 0, :], bre), (tb["bd"][:, 2, :], bim)])
            S.mm(xim, [(tb["bd"][:, 1, :], bre), (tb["bd"][:, 0, :], bim)])
            sink(b, xre, xim)

    def mix_even(self, i, layer):
        S, A, T, I = self.S, self.A, self.T, self.I
        self.prep(I["ab_w_in"][i], DC, [[(256 * b, 256)] for b in range(16)], self.WBM, 256)
        self.prep(I["ab_w_out"][i], DC, [[(256 * b, 256)] for b in range(8)], self.WBM2, 256)
        pw = I["pool_w"][i]
        pw2 = bass.AP(tensor=pw.tensor, offset=pw.offset, ap=[[256, 1024], [1, 256]])
        self.prep(pw2, 8, [[(0, 256)]], self.WBP, 256)
        gcol = self.col("gains", 16, 16 * (self.depth + layer))
        c_cw = self.cols["convw"][0] + 72 * i
        c_cb = self.cols["convb"][0] + 24 * i
        c_d = self.cols["hyd"][0] + 8 * i
        c_b4 = self.cols["b4"][0] + 16 * i
        c_mlp = self.cols["mlp"][0] + 8 * i
        c_nd = self.cols["ndelta"][0]
        c_ps = self.cols["pscale"][0] + 8 * i
        cv = self.cv
        for seq in self.seqs:
            L = seq["L"]
            tag = seq["name"]
            N2 = L // 64
            U3 = sv(self.U, 0, [[L, 32], [1, L]])
            PB3 = sv(self.PB, 0, [[L, 8], [1, L]])
            YB3 = sv(self.YB, 0, [[L, 8], [1, L]])
            A.top = 0
            h = A.alloc(BF16, DC, T)
            wslots = [A.alloc(BF16, DC, 256) for _ in range(3)]
            ust = [A.alloc(F32, 2, T) for _ in range(2)]
            xin = A.alloc(F32, DC, 512)
            sqb = A.alloc(BF16, DC, 512)
            rt = A.alloc(F32, 512)
            for t0 in range(0, L, T):
                self.group_norm(seq, t0, T, gcol, h, (xin, sqb, rt))
                for cb in range(16):
                    wb = wslots[cb % 3]
                    self.wget(wb, self.WBM, cb, DC, 256)
                    us = ust[cb % 2]
                    for jj in range(2):
                        for s_ in range(T // 512):
                            b = self.bank()
                            S.mm(b, [(wb[:, kc, 128 * jj:128 * (jj + 1)], h[:, kc, 512 * s_:512 * (s_ + 1)]) for kc in range(DC)])
                            S.copy(us[:, jj, 512 * s_:512 * (s_ + 1)], b, e=("act" if s_ % 2 else "dve"))
                    S.dma("sp", U3[:, 2 * cb:2 * cb + 2, t0:t0 + T], us)
            A.top = 0
            ub = A.alloc(F32, L + 16)
            t1 = A.alloc(F32, L + 16)
            t2 = A.alloc(F32, L + 16)
            pout = A.alloc(BF16, L)
            ec = A.alloc(F32, 64)
            e8 = A.alloc(F32, 16)
            S.dma("sp", ec, I[f"ec_{tag}"])
            S.memset(ub[:, 0:8], 0.0)
            S.memset(ub[:, L + 8:L + 16], 0.0)

            def R(buf, lo, hi):
                return buf[:, 8 + lo:8 + hi]

            for c in range(8):
                gi = c // 2
                w = POOL_WINDOWS[gi]
                S.dma("sp", R(ub, 0, L), U3[:, c, :])
                steps = {2: [(0, -1, 0)], 4: [(1, -1, 0), (0, -1, 1)], 8: [(3, -1, 0), (2, -1, 1), (0, -2, 2)],
                         16: [(7, -1, 0), (6, -1, 1), (4, -2, 2), (0, -4, 4)]}[w]
                src = ub
                dsts = [t1, t2]
                for li, (m, sa, sb) in enumerate(steps):
                    dst = dsts[li % 2]
                    S.tt(R(dst, -m, L + m), R(src, -m + sa, L + m + sa), R(src, -m + sb, L + m + sb), ALU.add)
                    src = dst
                ssum = src
                pf = dsts[len(steps) % 2]
                S.stt(R(pf, 0, L), R(ssum, 0, L), 1.0 / w, R(ub, 0, L), ALU.mult, ALU.subtract)
                for (lo, eo) in ((0, 0), (L - 8, 8)):
                    S.tt(e8[:, eo:eo + 8], R(ssum, lo, lo + 8), ec[:, 16 * gi + eo:16 * gi + eo + 8], ALU.mult)
                    S.tt(R(pf, lo, lo + 8), e8[:, eo:eo + 8], R(ub, lo, lo + 8), ALU.subtract)
                S.copy(pout, R(pf, 0, L), e="act")
                S.dma("sp", PB3[:, c, :], pout)
            A.top = 0
            tb = {}
            tb["tw1"] = A.alloc(F32, 384)
            tb["tw2"] = A.alloc(F32, 384)
            tb["bd"] = A.alloc(BF16, 3, 128)
            tb["cinv"] = A.alloc(BF16, 2, 256)
            S.dma("sp", tb["tw1"], I[f"tw1_{tag}"].rearrange("p a k -> p (a k)"))
            S.dma("sp", tb["tw2"], I[f"tw2_{tag}"].rearrange("p a k -> p (a k)"))
            S.dma("sp", tb["bd"], I[f"bd_{tag}"])
            S.dma("sp", tb["cinv"], I[f"cinv_{tag}"])
            w1 = A.alloc(F32, 64)
            w2 = A.alloc(F32, 64)
            w3 = A.alloc(F32, 64)
            w4 = A.alloc(F32, 2048)
            S.dma("sp", w1[0:33, :], I["hy_ff_w1"][i])
            S.dma("sp", w2[0:64, :], I["hy_ff_w2"][i])
            S.dma("sp", w3[0:64, :], I["hy_ff_w3"][i])
            S.dma("sp", w4[0:64, :], I["hy_ff_w4"][i])
            mark = A.top
            zt = A.alloc(F32, 512)
            ta = A.alloc(F32, 512)
            tn = A.alloc(F32, 512)
            hb = [A.alloc(F32, 512) for _ in range(2)]
            zext = I[f"zext_{tag}"]
            text = I[f"text_{tag}"]
            H3 = self.H3
            for j0 in range(0, 2 * L, 512):
                S.dma("sp", zt[0:33, :], zext[:, j0:j0 + 512])
                rhs = zt[0:33, :]
                for li, (wl, kk) in enumerate(((w1, 33), (w2, 64), (w3, 64))):
                    b = self.bank()
                    S.mm(b[0:64, :], [(wl[0:kk, 0:64], rhs)])
                    S.ts(ta[0:64, :], b[0:64, :], cv[0:64, c_mlp + 3:c_mlp + 4], cv[0:64, c_mlp + 4 + li:c_mlp + 5 + li],
                         ALU.mult, ALU.add)
                    S.ts(tn[0:64, :], ta[0:64, :], 1.0 / (2.0 * math.pi), 12582912.0, ALU.mult, ALU.add)
                    S.ts(tn[0:64, :], tn[0:64, :], -12582912.0, -2.0 * math.pi, ALU.add, ALU.mult)
                    S.tt(ta[0:64, :], ta[0:64, :], tn[0:64, :], ALU.add)
                    S.ts(ta[0:64, :], ta[0:64, :], 3.1415925, -3.1415925, ALU.min, ALU.max)
                    ho = hb[li % 2]
                    S.act(ho[0:64, :], ta[0:64, :], AF.Sin)
                    rhs = ho[0:64, :]
                S.dma("sp", H3[:, j0:j0 + 512], rhs)
            A.top = mark
            ld = A.alloc(F32, L + 2)
            x0c = A.alloc(BF16, L)
            vx = A.alloc(F32, L)
            xT = A.alloc(BF16, 128 * N2)
            yT = bass.AP(tensor=xT.tensor, offset=vx.offset * 2, ap=[list(xT.ap[0]), [1, 128 * N2]])
            Bt = A.alloc(BF16, 2, 4, 128)
            Yt = A.alloc(BF16, 2, 4, 128)
            Gt = A.alloc(BF16, 2, 4, 128)
            tP = A.alloc(F32, 512)
            tQ = A.alloc(F32, 512)
            kft = [A.alloc(F32, 2, 4, 128) for _ in range(2)]
            tm = [A.alloc(F32, 512) for _ in range(4)]
            h3t = A.alloc(F32, 512)
            txt = A.alloc(F32, 512)
            dec = A.alloc(F32, 512)
            ksum = A.alloc(F32, 64)
            rn = A.alloc(F32, 4)
            cbuf = bass.AP(tensor=vx.tensor, offset=(xT.offset // 2), ap=[list(vx.ap[0]), [1, L]])
            kbuf = bass.AP(tensor=xT.tensor, offset=ld.offset * 2, ap=[list(xT.ap[0]), [1, 2 * L]])
            yb = xT[:, 0:L]
            S.memset(ld[:, 0:1], 0.0)
            S.memset(ld[:, L + 1:L + 2], 0.0)
            KFv = self.KF
            nt = (2 * L) // 512

            def conv3(j, out, first_out=None):
                S.memset(ld[:, 0:1], 0.0)
                S.dma("sp", ld[:, 1:L + 1], U3[:, 8 + j, :])
                acc = first_out if first_out is not None else out
                S.ts(acc, ld[:, 1:L + 1], cv[:, c_cw + 24 + j:c_cw + 25 + j], cv[:, c_cb + j:c_cb + j + 1], ALU.mult, ALU.add)
                S.stt(acc, ld[:, 0:L], cv[:, c_cw + j:c_cw + j + 1], acc, ALU.mult, ALU.add)
                S.stt(out, ld[:, 2:L + 2], cv[:, c_cw + 48 + j:c_cw + 49 + j], acc, ALU.mult, ALU.add)

            for c in range(8):
                for ti in range(nt):
                    j0 = 512 * ti
                    S.dma("sp", h3t[0:64, :], H3[:, j0:j0 + 512])
                    S.dma("sp", txt, bass.AP(tensor=text.tensor, offset=text.offset + j0, ap=[[0, 128], [1, 512]]))
                    S.act(dec, txt, AF.Exp, scale=cv[:, c_nd + c:c_nd + c + 1])
                    back = j0 >= L
                    wc = 1024 * back + 128 * c
                    b = self.bank()
                    S.mm(b, [(w4[0:64, wc:wc + 128], h3t[0:64, :])])
                    kf32 = tm[ti % 2]
                    S.stt(kf32, b, cv[:, c_b4 + 8 * back + c:c_b4 + 8 * back + c + 1], dec, ALU.add, ALU.mult)
                    if j0 == L:
                        S.memset(kf32[:, 0:1], 0.0)
                    S.reduce(ksum[:, ti:ti + 1], kf32, ALU.add, absval=True)
                    S.copy(kbuf[:, j0:j0 + 512], kf32, e="act")
                S.reduce(rn[:, 0:1], ksum[:, 0:nt], ALU.add)
                S.recip(rn[:, 1:2], rn[:, 0:1])
                for q4 in range(N2 // 4):
                    bb = self.bankb(self.bankrr)
                    self.bankrr = (self.bankrr + 1) % 8
                    S.tr([(bb[:, 128 * q:128 * (q + 1)], sv(kbuf, 4 * q4 + q, [[N2, 128]])) for q in range(4)], self.identb)
                    S.copy(sv(xT, 4 * q4, [[N2, 128], [1, 4]]), sv(bb, 0, [[1, 128], [128, 4]]), e=("act" if q4 % 2 else "dve"))

                def fsink(b, xre, xim):
                    kt = kft[b % 2]
                    S.copy(kt[:, 0, :, :].rearrange("p g k -> p (g k)"), xre, e="act")
                    S.copy(kt[:, 1, :, :].rearrange("p g k -> p (g k)"), xim, e="dve")
                    S.dma("sp", bass.AP(tensor=KFv.tensor, offset=KFv.offset + 1024 * b, ap=[[N2 * 256, 128], [1, 1024]]),
                          kt.rearrange("p a g k -> p (a g k)"))

                self.fft_fwd(xT, 128, N2, tb, (Bt, tP, tQ), fsink)
                conv3(16 + c, vx)
                conv3(8 + c, cbuf)
                S.tt(vx, vx, cbuf, ALU.mult)
                conv3(c, x0c, first_out=cbuf)
                S.ts(ld[:, 1:L + 1], vx, cv[:, c_d + c:c_d + c + 1], 0.0, ALU.mult, ALU.add)
                z1 = ld[:, 1:L + 1]
                for q4 in range(N2 // 4):
                    bk = self.bank()
                    S.tr([(bk[0:64, 128 * q:128 * (q + 1)], sv(vx, 4 * q4 + q, [[N2, 64]])) for q in range(4)], self.ident)
                    S.copy(sv(xT[0:64, :], 4 * q4, [[N2, 128], [1, 4]]), sv(bk[0:64, :], 0, [[1, 128], [128, 4]]),
                           e=("act" if q4 % 2 else "dve"))

                def dsink(b, xre, xim):
                    kt = kft[b % 2]
                    S.dma("sp", kt.rearrange("p a g k -> p (a g k)"),
                          bass.AP(tensor=KFv.tensor, offset=KFv.offset + 1024 * b, ap=[[N2 * 256, 128], [1, 1024]]))
                    kre = kt[:, 0, :, :].rearrange("p g k -> p (g k)")
                    kim = kt[:, 1, :, :].rearrange("p g k -> p (g k)")
                    S.tt(tm[0], xre, kre, ALU.mult)
                    S.tt(tm[1], xim, kim, ALU.mult)
                    S.tt(Yt[:, 0, :, :].rearrange("p g k -> p (g k)"), tm[0], tm[1], ALU.subtract)
                    S.tt(tm[2], xre, kim, ALU.mult)
                    S.tt(tm[3], xim, kre, ALU.mult)
                    S.tt(Yt[:, 1, :, :].rearrange("p g k -> p (g k)"), tm[2], tm[3], ALU.add)
                    for k in range(2):
                        bk = self.bank()
                        for g2 in range(2):
                            gi = 2 * k + g2
                            S.mm(bk[:, 256 * g2:256 * (g2 + 1)], [(Yt[:, 0, gi, :], tb["cinv"][:, 0, :]),
                                                                  (Yt[:, 1, gi, :], tb["cinv"][:, 1, :])])
                        self.cmul(bk, tb["tw2"], Gt, k, tP, tQ)
                    be = self.bank()
                    S.mm(be[0:64, :], [(self.etab[:, 0, :], Gt[:, 0, :, :].rearrange("p g k -> p (g k)")),
                                       (self.etab[:, 1, :], Gt[:, 1, :, :].rearrange("p g k -> p (g k)"))])
                    S.copy(yT[0:64, 512 * b:512 * (b + 1)], be[0:64, :], e="act")

                self.fft_fwd(xT, 64, N2, tb, (Bt, tP, tQ), dsink)
                for j in range(N2 // 8):
                    bb = self.bankb(self.bankrr)
                    self.bankrr = (self.bankrr + 1) % 8
                    S.tr([(bb[:, 64 * q:64 * (q + 1)], sv(yT[0:64, :], 8 * j + q, [[N2, 128]])) for q in range(8)], self.identb)
                    tf = tm[j % 2]
                    S.stt(sv(tf, 0, [[64, 8], [1, 64]]), sv(bb, 0, [[64, 8], [1, 64]]), rn[:, 1:2],
                          sv(z1, 8 * j, [[1, 8], [N2, 64]]), ALU.mult, ALU.add)
                    S.tt(sv(yb, 8 * j, [[1, 8], [N2, 64]]), sv(tf, 0, [[64, 8], [1, 64]]),
                         sv(x0c, 8 * j, [[1, 8], [N2, 64]]), ALU.mult)
                S.dma("sp", YB3[:, c, :], yb)
            A.top = 0
            cat = A.alloc(BF16, DC, T)
            pin = A.alloc(BF16, 8, T)
            wp = A.alloc(BF16, 8, 256)
            woslots = [A.alloc(BF16, DC, 256) for _ in range(2)]
            xsl = [A.alloc(F32, 2, T) for _ in range(2)]
            xo = [A.alloc(F32, 2, T) for _ in range(2)]
            self.wget(wp, self.WBP, 0, 8, 256)
            for t0 in range(0, L, T):
                S.dma("sp", pin, PB3[:, :, t0:t0 + T])
                S.dma("sp", cat[:, 8:16, :], YB3[:, :, t0:t0 + T])
                for g in range(4):
                    for oc in range(2):
                        for s_ in range(T // 512):
                            b = self.bank()
                            S.mm(b, [(wp[:, 2 * g + kc, 128 * oc:128 * (oc + 1)], pin[:, 2 * g + kc, 512 * s_:512 * (s_ + 1)])
                                     for kc in range(2)])
                            cc = c_ps + 2 * g + oc
                            S.ts(cat[:, 2 * g + oc, 512 * s_:512 * (s_ + 1)], b, cv[:, cc:cc + 1], 0.0, ALU.mult, ALU.add)
                self.out_proj(seq, t0, T, cat, DC, self.WBM2, 1.0, woslots, xsl, xo)


def pack_cvec(inp, depth):
    n_even, n_odd = (depth + 1) // 2, depth // 2
    cols = {}
    parts = []
    pos = [0]

    def add(name, arr):
        arr = np.ascontiguousarray(arr, dtype=np.float32)
        cols[name] = (pos[0], arr.shape[1])
        parts.append(arr)
        pos[0] += arr.shape[1]

    g = [fm(inp["norm_ffn1"][l], 16) for l in range(depth)] + [fm(inp["norm_mix"][l], 16) for l in range(depth)] \
        + [fm(inp["norm_ffn2"][l], 16) for l in range(depth)] + [fm(inp["norm_final"], 16)]
    add("gains", np.concatenate(g, 1))
    z = np.zeros((128, 1), np.float32)
    if n_even:
        add("pscale", np.concatenate([fm(inp["pool_scale"][i], 8) for i in range(n_even)], 1))
        add("convw", np.concatenate([fm(inp["hy_conv_w"][i][k], 24) for i in range(n_even) for k in range(3)], 1))
        add("convb", np.concatenate([fm(inp["hy_conv_b"][i], 24) for i in range(n_even)], 1))
        add("hyd", np.concatenate([fm(inp["hy_d"][i], 8) for i in range(n_even)], 1))
        add("b4", np.concatenate([fm(inp["hy_ff_b4"][i], 16) for i in range(n_even)], 1))
        m = np.zeros((128, 8 * n_even), np.float32)
        for i in range(n_even):
            for k, nm in enumerate(("hy_ff_b1", "hy_ff_b2", "hy_ff_b3", "hy_freq")):
                m[0:64, 8 * i + k] = inp[nm][i]
        add("mlp", m)
    else:
        add("mlp", np.zeros((128, 8), np.float32))
    if n_odd:
        add("sink", np.concatenate([np.broadcast_to(np.asarray(inp["attn_sink"][i], np.float32)[None, :], (128, 16))
                                    for i in range(n_odd)], 1))
    mind = math.log(1e-2) / 1.5
    maxd = math.log(1e-2) / 0.3
    deltas = np.linspace(mind, maxd, CH, dtype=np.float32)
    add("ndelta", -np.abs(fm(deltas, 8)))
    return np.concatenate(parts, 1), cols


_CACHE = {}


def kernel(**inputs):
    cfg = dict(CFG)
    cfg.update(inputs.pop("_cfg", {}))
    inp = {k: np.asarray(v) for k, v in inputs.items()}
    depth = cfg["depth"]
    Lp, Ls = cfg["Lp"], cfg["Ls"]
    cvec, cols = pack_cvec(inp, depth)
    ct = common_tables()
    host = dict(cvec=cvec, ident=ct["ident"], identb=ct["identb"], f1=ct["f1"], etab=ct["e"], oh=ct["oh"],
                mask=ct["mask"], jrev=ct["jrev"])
    for tag, L in (("p", Lp), ("s", Ls)):
        ft = fft_tables(L)
        for k in ("tw1", "tw2", "bd", "cinv", "zext", "text"):
            host[f"{k}_{tag}"] = ft[k]
        host[f"ec_{tag}"] = pool_edges(L)
    for k in ("ffn1_wi", "ffn1_wo", "ffn2_wi", "ffn2_wo", "ab_w_in", "ab_w_out", "pool_w", "attn_w_qkv", "attn_w_o",
              "rel_bias", "hy_ff_w1", "hy_ff_w2", "hy_ff_w3", "hy_ff_w4"):
        plan = cfg.get("plan")
        if plan is not None and k.startswith("ffn") and not any(p[0] == "ffn" and f"ffn{p[1]}" == k[:4] for p in plan):
            continue
        if inp[k].size > 0:
            host[k] = np.ascontiguousarray(inp[k], dtype=np.float32)
    xp = np.ascontiguousarray(inp["x_prompt"][0], dtype=np.float32)
    xs = np.asarray(inp["x_sample"], dtype=np.float32)
    nb = xs.shape[0]
    host["xp"] = xp
    host["xs"] = np.ascontiguousarray(xs[0])
    import time as _time
    _t0 = _time.time()
    nc = bass.Bass("TRN2", target_bir_lowering=False)
    prog = Prog(cfg, cols, cvec.shape[1])
    prog.build(nc, host)
    print(f"[kernel] build {_time.time() - _t0:.1f}s ops={prog.S.nops} sems={prog.S.nsem}", flush=True)
    _t0 = _time.time()
    if cfg.get("build_only"):
        return None
    in_maps = []
    for c in range(8):
        m = dict(host)
        m["xs"] = np.ascontiguousarray(xs[c % nb])
        in_maps.append(m)
    res = run_bass_kernel_spmd(nc, in_maps, core_ids=list(range(8)))
    print(f"[kernel] run {_time.time() - _t0:.1f}s", flush=True)
    yp = np.asarray(res.results[0]["yp"], dtype=np.float32)[None]
    ys = np.stack([np.asarray(res.results[c]["ys"], dtype=np.float32) for c in range(nb)], 0)
    return (yp, ys)
```

```python
import math
from contextlib import ExitStack

import numpy as np
import ml_dtypes

import concourse.bass as bass
import concourse.mybir as mybir
from concourse.bass_utils import run_bass_kernel_spmd

F32 = mybir.dt.float32
BF16 = mybir.dt.bfloat16
AF = mybir.ActivationFunctionType
ALU = mybir.AluOpType
AX = mybir.AxisListType

D = 2048
DC = 16
DFF = 5632
FC = 44
NH = 16
NKV = 4
HD = 128
EPS = 1e-6
POOL_WINDOWS = (2, 4, 8, 16)
CH = 1024
NEG = -30000.0

CFG = dict(Lp=8192, Ls=2048, depth=4, T=1024, plan=None)


def _esize(dt):
    return 4 if dt == F32 else 2


class Sched:
    SEM_LIMIT = 24000

    def __init__(self, nc, stack):
        self.nc = nc
        self.stack = stack
        self.eng = {"pe": nc.tensor, "act": nc.scalar, "dve": nc.vector, "pool": nc.gpsimd, "sp": nc.sync}
        self.nsem = 0
        self.csem = {}
        self.observed = {e: {} for e in self.eng}
        self.records = {}
        self.dma_slots = {}
        self.dma_rr = {}
        for e in ("sp", "pool", "act"):
            self.dma_slots[e] = [[self.new_sem(), 0] for _ in range(8)]
            self.dma_rr[e] = 0
        self.nops = 0
        self.deferred = []

    def new_sem(self):
        self.nsem += 1
        return self.stack.enter_context(self.nc.semaphore(f"s{self.nsem}"))

    @staticmethod
    def rng(ap):
        t = ap.tensor
        es = _esize(ap.dtype)
        pairs = ap.ap
        off = ap.offset
        kind = type(t).__name__
        if "DRam" in kind:
            lo = off
            hi = off
            for st, cnt in pairs:
                if st >= 0:
                    hi += st * (cnt - 1)
                else:
                    lo += st * (cnt - 1)
            return (t.name, lo * es, (hi + 1) * es)
        rowlen = pairs[0][0]
        f = off % rowlen if rowlen > 0 else off
        lo = f
        hi = f
        for st, cnt in pairs[1:]:
            if st >= 0:
                hi += st * (cnt - 1)
            else:
                lo += st * (cnt - 1)
        return (t.name, lo * es, (hi + 1) * es)

    def _wait(self, e, sem, val):
        key = id(sem)
        ob = self.observed[e]
        if ob.get(key, 0) >= val:
            return
        self.eng[e].wait_ge(sem, val)
        ob[key] = val

    def op(self, e, fn, reads=(), writes=(), dma=False, sig=True):
        self.nops += 1
        rr = [self.rng(a) for a in reads]
        ww = [self.rng(a) for a in writes]
        deps = {}
        for sp, lo, hi in rr:
            for rec in self.records.get(sp, ()):
                if rec[4] and rec[0] < hi and lo < rec[1]:
                    k = id(rec[2])
                    if deps.get(k, (None, 0))[1] < rec[3]:
                        deps[k] = (rec[2], rec[3])
        for sp, lo, hi in ww:
            for rec in self.records.get(sp, ()):
                if rec[0] < hi and lo < rec[1]:
                    k = id(rec[2])
                    if deps.get(k, (None, 0))[1] < rec[3]:
                        deps[k] = (rec[2], rec[3])
        if dma:
            slots = self.dma_slots[e]
            si = self.dma_rr[e]
            self.dma_rr[e] = (si + 1) % len(slots)
            slot = slots[si]
            if slot[1] >= self.SEM_LIMIT:
                slot[0] = self.new_sem()
                slot[1] = 0
            elif slot[1] > 0:
                k = id(slot[0])
                if deps.get(k, (None, 0))[1] < slot[1]:
                    deps[k] = (slot[0], slot[1])
        for sem, val in deps.values():
            self._wait(e, sem, val)
        ins = fn()
        if dma:
            slot[1] += 16
            ins.then_inc(slot[0], 16)
            ev = (slot[0], slot[1])
        else:
            cs = self.csem.get(e)
            if cs is None or cs[1] >= self.SEM_LIMIT:
                cs = [self.new_sem(), 0]
                self.csem[e] = cs
            cs[1] += 1
            ins.then_inc(cs[0], 1)
            ev = (cs[0], cs[1])
        for sp, lo, hi in ww:
            lst = self.records.setdefault(sp, [])
            lst[:] = [r for r in lst if not (lo <= r[0] and r[1] <= hi)]
            lst.append([lo, hi, ev[0], ev[1], True])
        for sp, lo, hi in rr:
            lst = self.records.setdefault(sp, [])
            lst[:] = [r for r in lst if not ((not r[4]) and r[2] is ev[0] and lo <= r[0] and r[1] <= hi)]
            lst.append([lo, hi, ev[0], ev[1], False])
        return ins

    def mm(self, out, pairs):
        n = len(pairs)
        reads = [a for p in pairs for a in p]

        def fn():
            ins = None
            for i, (l, r) in enumerate(pairs):
                ins = self.nc.tensor.matmul(out, l, r, start=(i == 0), stop=(i == n - 1))
            return ins

        return self.op("pe", fn, reads=reads, writes=[out])

    def tr(self, items, ident):
        reads = [i for _, i in items] + [ident]
        writes = [o for o, _ in items]

        def fn():
            ins = None
            for o, i in items:
                k = i.shape[0]
                ins = self.nc.tensor.transpose(o, i, ident[0:k, 0:k])
            return ins

        return self.op("pe", fn, reads=reads, writes=writes)

    def dma(self, e, out, in_, defer=False):
        if defer:
            self.deferred.append((e, out, in_))
            return None
        return self.op(e, lambda: self.eng[e].dma_start(out=out, in_=in_), reads=[in_], writes=[out], dma=True)

    def flush(self):
        d, self.deferred = self.deferred, []
        for e, out, in_ in d:
            self.dma(e, out, in_)

    def act(self, out, in_, func, bias=None, scale=None, accum_out=None, e="act"):
        reads = [in_]
        kw = {}
        if bias is not None:
            kw["bias"] = bias
            if not isinstance(bias, (int, float)):
                reads.append(bias)
        if scale is not None:
            kw["scale"] = scale
            if not isinstance(scale, (int, float)):
                reads.append(scale)
        writes = [out]
        if accum_out is not None:
            kw["accum_out"] = accum_out
            writes.append(accum_out)
        return self.op("act", lambda: self.nc.scalar.activation(out, in_, func, **kw), reads=reads, writes=writes)

    def tt(self, out, in0, in1, op, e="dve"):
        return self.op(e, lambda: self.eng[e].tensor_tensor(out, in0, in1, op), reads=[in0, in1], writes=[out])

    def ts(self, out, in0, s1, s2, op0, op1=None, e="dve"):
        reads = [in0] + [s for s in (s1, s2) if s is not None and not isinstance(s, (int, float))]
        if op1 is None:
            return self.op(e, lambda: self.eng[e].tensor_scalar(out, in0, s1, None, op0), reads=reads, writes=[out])
        return self.op(e, lambda: self.eng[e].tensor_scalar(out, in0, s1, s2, op0, op1), reads=reads, writes=[out])

    def stt(self, out, in0, scalar, in1, op0, op1, e="dve"):
        reads = [in0, in1] + ([] if isinstance(scalar, (int, float)) else [scalar])
        return self.op(e, lambda: self.eng[e].scalar_tensor_tensor(out, in0, scalar, in1, op0, op1),
                       reads=reads, writes=[out])

    def copy(self, out, in_, e="dve"):
        if e == "act":
            return self.op("act", lambda: self.nc.scalar.copy(out, in_), reads=[in_], writes=[out])
        return self.op(e, lambda: self.eng[e].tensor_copy(out, in_), reads=[in_], writes=[out])

    def memset(self, out, val, e="dve"):
        return self.op(e, lambda: self.eng[e].memset(out, val), reads=[], writes=[out])

    def reduce(self, out, in_, op, absval=False, e="dve"):
        kw = {"apply_absolute_value": True} if absval else {}
        return self.op(e, lambda: self.eng[e].tensor_reduce(out, in_, AX.X, op, **kw), reads=[in_], writes=[out])

    def recip(self, out, in_):
        return self.op("dve", lambda: self.nc.vector.reciprocal(out, in_), reads=[in_], writes=[out])

    def drain(self):
        self.flush()
        for e, cs in self.csem.items():
            self._wait("sp", cs[0], cs[1])
        for e, slots in self.dma_slots.items():
            for sem, val in slots:
                if val > 0:
                    self._wait("sp", sem, val)


def sv(ap, off, dims):
    return bass.AP(tensor=ap.tensor, offset=ap.offset + off, ap=[list(ap.ap[0])] + [list(d) for d in dims])


def fft_tables(L):
    N = 2 * L
    N2 = N // 128
    G = 128 // N2
    n2 = np.arange(N2)
    k1 = np.arange(128)
    ang = 2 * np.pi * np.outer(n2, k1) / N
    c = np.tile(np.cos(ang), (G, 1))
    d = np.tile(-np.sin(ang), (G, 1))
    tw1 = np.stack([c, d, -d], axis=1).astype(np.float32)
    ang2 = 2 * np.pi * np.outer(k1, n2) / N
    c2 = np.tile(np.cos(ang2), (1, G)) / N
    d2 = np.tile(np.sin(ang2), (1, G)) / N
    tw2 = np.stack([c2, d2, -d2], axis=1).astype(np.float32)
    a = 2 * np.pi * np.outer(n2, n2) / N2
    bre = np.kron(np.eye(G), np.cos(a))
    bim = np.kron(np.eye(G), -np.sin(a))
    bd = np.stack([bre, bim, -bim], axis=1).astype(ml_dtypes.bfloat16)
    cre = np.kron(np.eye(G), np.cos(a))
    cim = np.kron(np.eye(G), np.sin(a))
    cinv = np.stack([np.concatenate([cre, cim], 1), np.concatenate([-cim, cre], 1)], axis=1)
    cinv = cinv.astype(ml_dtypes.bfloat16)
    f32 = np.float32
    t = np.linspace(0.0, 1.0, L, dtype=f32)
    w = (2.0 * math.pi * np.arange(L, dtype=f32) / L).astype(f32)
    f = np.linspace(1e-4, 15, 16, dtype=f32)
    fw = (f[None, :] * w[:, None]).astype(f32)
    z = np.concatenate([t[:, None], np.cos(fw), -np.sin(fw)], axis=-1).astype(f32)
    pos = np.concatenate([np.arange(L), [0], L - np.arange(1, L)])
    zext = np.ascontiguousarray(z[pos].T).astype(f32)
    text = t[pos].astype(f32)
    return dict(N2=N2, G=G, tw1=tw1, tw2=tw2, bd=bd, cinv=cinv, zext=zext, text=text.reshape(1, -1))


def common_tables():
    n1 = np.arange(128)
    a = 2 * np.pi * np.outer(n1, n1) / 128
    f1 = np.concatenate([np.cos(a), -np.sin(a)], axis=1).astype(ml_dtypes.bfloat16)
    e = np.stack([np.cos(a)[:, :64], -np.sin(a)[:, :64]], axis=1).astype(ml_dtypes.bfloat16)
    ident = np.eye(128, dtype=np.float32)
    jrev = np.ascontiguousarray(np.eye(128, dtype=np.float32)[::-1])
    rel = np.arange(-255, 257)
    nb = 16
    me = 8
    n = np.abs(rel)
    large = me + (np.log(np.maximum(n, 1).astype(np.float32) / me) / math.log(128 / me) * (nb - me)).astype(np.int32)
    large = np.minimum(large, nb - 1)
    bucket = (rel > 0).astype(np.int32) * nb + np.where(n < me, n, large)
    valid = n <= 128
    oh = np.zeros((32, 512), np.float32)
    oh[bucket[valid], np.nonzero(valid)[0]] = 1.0
    mask = np.where(valid, 0.0, NEG).astype(np.float32).reshape(1, 512)
    return dict(f1=f1, e=e, ident=ident, identb=ident.astype(ml_dtypes.bfloat16), jrev=jrev, oh=oh, mask=mask)


def pool_edges(L):
    ec = np.zeros((4, 16), np.float32)
    for g, w in enumerate(POOL_WINDOWS):
        t = np.concatenate([np.arange(8), np.arange(L - 8, L)])
        lo = np.clip(t - w // 2, 0, L)
        hi = np.clip(t + w // 2, 0, L)
        ec[g] = 1.0 / (hi - lo)
    return np.ascontiguousarray(np.broadcast_to(ec.reshape(1, 64), (128, 64)))


def fm(v, nchunk):
    return np.ascontiguousarray(np.asarray(v, np.float32).reshape(nchunk, 128).T)


class Arena:
    def __init__(self, ap, nbytes):
        self.ap = ap
        self.nbytes = nbytes
        self.top = 0

    def alloc(self, dt, *shape):
        n = 1
        for s in shape:
            n *= s
        nb = n * _esize(dt)
        nb = (nb + 63) // 64 * 64
        off = self.top
        self.top += nb
        assert self.top <= self.nbytes, f"SBUF arena overflow {self.top} > {self.nbytes}"
        v = self.ap[:, off // 4:(off + nb) // 4]
        if dt != F32:
            v = v.bitcast(dt)
        v = v[:, 0:n]
        if len(shape) == 2:
            v = v.rearrange("p (a b) -> p a b", a=shape[0])
        elif len(shape) == 3:
            v = v.rearrange("p (a b c) -> p a b c", a=shape[0], b=shape[1])
        return v


class Prog:
    def __init__(self, cfg, cols, ncol):
        self.cfg = cfg
        self.cols = cols
        self.ncol = ncol
        self.Lp, self.Ls, self.depth, self.T = cfg["Lp"], cfg["Ls"], cfg["depth"], cfg["T"]
        self.n_even = (self.depth + 1) // 2
        self.n_odd = self.depth // 2

    def declare(self, nc, host):
        self.nc = nc
        self.I = {}
        for name, arr in host.items():
            dt = F32 if arr.dtype == np.float32 else BF16
            self.I[name] = nc.dram_tensor(name, list(arr.shape), dt, kind="ExternalInput").ap()
        Lp, Ls = self.Lp, self.Ls
        Lm = max(Lp, Ls)
        self.O = {
            "yp": nc.dram_tensor("yp", [Lp, D], F32, kind="ExternalOutput").ap(),
            "ys": nc.dram_tensor("ys", [Ls, D], F32, kind="ExternalOutput").ap(),
        }

        def scr(name, shape, dt):
            return nc.dram_tensor(name, shape, dt, kind="Internal").ap()

        self.seqs = [
            dict(name="p", x=self.I["xp"], y=self.O["yp"], L=Lp, XS=scr("XSp", [128, DC * Lp], F32)),
            dict(name="s", x=self.I["xs"], y=self.O["ys"], L=Ls, XS=scr("XSs", [128, DC * Ls], F32)),
        ]
        self.U = scr("U", [128, 32 * Lm], F32)
        self.PB = scr("PB", [128, 8 * Lm], BF16)
        self.YB = scr("YB", [128, 8 * Lm], BF16)
        self.KF = scr("KF", [128, (Lm // 64) * 256], F32)
        self.QT = scr("QT", [128, 16 * Lm], BF16)
        self.KT = scr("KT", [128, 4 * Lm], BF16)
        self.VT = scr("VT", [128, (Lm // 128) * 512], BF16)
        self.OT = scr("OT", [128, 16 * Lm], BF16)
        self.TB = scr("TB", [16, 512], F32)
        self.WBI = scr("WBI", [128, (FC // 2) * DC * 512], BF16)
        self.WBO = scr("WBO", [128, 8 * FC * 256], BF16)
        self.WBM = scr("WBM", [128, 16 * DC * 256], BF16)
        self.WBM2 = scr("WBM2", [128, 8 * DC * 256], BF16)
        self.WBP = scr("WBP", [128, 8 * 256], BF16)
        self.H3 = scr("H3", [64, 2 * Lm], F32)
        self.BIAS = scr("BIAS", [128, 16 * 384], F32)

    def build(self, nc, host):
        self.declare(nc, host)
        with ExitStack() as stack:
            SBW = 45056
            arena_t = stack.enter_context(nc.sbuf_tensor("arena", [128, SBW], F32))
            const_t = stack.enter_context(nc.sbuf_tensor("consts", [128, 3072], F32))
            ps_t = stack.enter_context(nc.psum_tensor("ps", [128, 4096], F32))
            self.S = Sched(nc, stack)
            self.A = Arena(arena_t[:, :], SBW * 4)
            self.CA = Arena(const_t[:, :], 3072 * 4)
            self.ps = ps_t[:, :]
            self.psb = ps_t[:, :].bitcast(BF16)
            self.bankrr = 0
            self.load_consts()
            for seq in self.seqs:
                self.convert_in(seq)
            plan = self.cfg.get("plan")
            if plan is None:
                plan = []
                for l in range(self.depth):
                    plan += [("ffn", 1, l), ("mix", l), ("ffn", 2, l)]
            for st in plan:
                if st[0] == "ffn":
                    self.ffn(st[1], st[2])
                elif st[1] % 2 == 0:
                    self.mix_even(st[1] // 2, st[1])
                else:
                    self.mix_odd(st[1] // 2, st[1])
            for seq in self.seqs:
                self.final_out(seq)
            self.S.drain()
        return nc

    def bank(self, i=None):
        if i is None:
            i = self.bankrr
            self.bankrr = (self.bankrr + 1) % 8
        return self.ps[:, 512 * i:512 * (i + 1)]

    def bankb(self, i):
        return self.psb[:, 1024 * i:1024 * (i + 1)]

    def col(self, name, n=None, i=0):
        c0, w = self.cols[name]
        if n is None:
            return self.cv[:, c0:c0 + w]
        return self.cv[:, c0 + i:c0 + i + n]

    def load_consts(self):
        S, CA, I = self.S, self.CA, self.I
        self.cv = CA.alloc(F32, self.ncol)
        S.dma("sp", self.cv, I["cvec"])
        self.ident = CA.alloc(F32, 128)
        S.dma("sp", self.ident, I["ident"])
        self.identb = CA.alloc(BF16, 128)
        S.dma("sp", self.identb, I["identb"])
        self.ones = CA.alloc(BF16, 128)
        S.memset(self.ones, 1.0)
        self.f1 = CA.alloc(BF16, 256)
        S.dma("sp", self.f1, I["f1"])
        self.etab = CA.alloc(BF16, 2, 64)
        S.dma("sp", self.etab, I["etab"])
        g0, gw = self.cols["gains"]
        S.ts(self.cv[:, g0:g0 + gw], self.cv[:, g0:g0 + gw], math.sqrt(D), 0.0, ALU.mult, ALU.add)
        self.epsc = CA.alloc(F32, 4)
        S.memset(self.epsc, float(D * EPS))
        for i in range(self.n_even):
            for k in range(3):
                c0 = self.cols["mlp"][0] + 8 * i
                S.ts(self.cv[0:64, c0 + 4 + k:c0 + 5 + k], self.cv[0:64, c0 + k:c0 + k + 1],
                     self.cv[0:64, c0 + 3:c0 + 4], 0.0, ALU.mult, ALU.add)

    def convert_in(self, seq):
        S, A = self.S, self.A
        L = seq["L"]
        XS = seq["XS"].rearrange("p (c t) -> p c t", c=DC)
        A.top = 0
        xin = A.alloc(F32, 4, D)
        xo = A.alloc(F32, DC, 512)
        for t0 in range(0, L, 512):
            for k in range(4):
                S.dma("sp", xin[:, k, :], seq["x"][t0 + 128 * k:t0 + 128 * (k + 1), :])
            for c in range(DC):
                b = self.bank()
                S.tr([(b[:, 128 * k:128 * (k + 1)], xin[:, k, 128 * c:128 * (c + 1)]) for k in range(4)], self.ident)
                S.copy(xo[:, c, :], b, e=("act" if c % 2 else "dve"))
            S.dma("sp", XS[:, :, t0:t0 + 512], xo)

    def norm_tile(self, XSv, t0, gcol, xin, sqb, rt, out_fn):
        S = self.S
        S.dma("sp", xin, XSv[:, :, t0:t0 + 512])
        S.act(sqb, xin, AF.Square)
        b = self.bank()
        S.mm(b, [(self.ones, sqb[:, c, :]) for c in range(DC)])
        S.act(rt, b, AF.Ln, bias=self.epsc[:, 0:1])
        S.act(rt, rt, AF.Exp, scale=-0.5)
        for c in range(DC):
            out_fn(c, xin[:, c, :], rt, gcol[:, c:c + 1])

    def final_out(self, seq):
        S, A = self.S, self.A
        L = seq["L"]
        XS = seq["XS"].rearrange("p (c t) -> p c t", c=DC)
        gcol = self.col("gains", 16, 16 * 3 * self.depth)
        A.top = 0
        xin = A.alloc(F32, DC, 512)
        sqb = A.alloc(BF16, DC, 512)
        rt = A.alloc(F32, 512)
        hf = A.alloc(F32, DC, 512)
        yo = A.alloc(F32, 4, D)
        for t0 in range(0, L, 512):
            self.norm_tile(XS, t0, gcol, xin, sqb, rt,
                           lambda c, xc, r, g: S.stt(hf[:, c, :], xc, g, r, ALU.mult, ALU.mult))
            for k in range(4):
                for cq in range(4):
                    b = self.bank()
                    S.tr([(b[:, 128 * j:128 * (j + 1)], hf[:, 4 * cq + j, 128 * k:128 * (k + 1)]) for j in range(4)],
                         self.ident)
                    S.copy(yo[:, k, 512 * cq:512 * (cq + 1)], b, e=("act" if cq % 2 else "dve"))
            for k in range(4):
                S.dma("sp", seq["y"][t0 + 128 * k:t0 + 128 * (k + 1), :], yo[:, k, :])

    def prep(self, W, nk, col_blocks, dst, bw):
        S, A = self.S, self.A
        A.top = 0
        KH = 16 if nk <= 16 else (nk + 1) // 2
        stg = [A.alloc(F32, KH, 256) for _ in range(2)]
        ob = [A.alloc(BF16, nk, bw) for _ in range(2)]
        ncw = W.shape[1]
        n = 0
        for b, runs in enumerate(col_blocks):
            o = ob[b % 2]
            j0 = 0
            for (c0, w) in runs:
                for k0 in range(0, nk, KH):
                    kn = min(KH, nk - k0)
                    st = stg[n % 2]
                    src = bass.AP(tensor=W.tensor, offset=W.offset + (k0 * 128) * ncw + c0,
                                  ap=[[ncw, 128], [128 * ncw, kn], [1, w]])
                    S.dma("sp", st[:, 0:kn, 0:w], src)
                    S.copy(o[:, k0:k0 + kn, j0:j0 + w], st[:, 0:kn, 0:w], e=("act" if n % 2 else "dve"))
                    n += 1
                j0 += w
            S.flush()
            S.dma("sp", bass.AP(tensor=dst.tensor, offset=dst.offset + b * 128 * nk * bw, ap=[[nk * bw, 128], [1, nk * bw]]),
                  o.rearrange("p k j -> p (k j)"), defer=True)
        S.flush()

    def wget(self, slot, scr, b, nk, bw):
        src = bass.AP(tensor=scr.tensor, offset=scr.offset + b * 128 * nk * bw, ap=[[nk * bw, 128], [1, nk * bw]])
        self.S.dma("sp", slot.rearrange("p k j -> p (k j)"), src)

    def group_norm(self, seq, t0, T, gcol, h, scratch):
        S = self.S
        XS = seq["XS"].rearrange("p (c t) -> p c t", c=DC)
        xin, sqb, rt = scratch
        xins = xin if isinstance(xin, list) else [xin]
        for s in range(T // 512):
            self.norm_tile(XS, t0 + 512 * s, gcol, xins[s % len(xins)], sqb, rt,
                           lambda c, xc, r, g, s=s: S.stt(h[:, c, 512 * s:512 * (s + 1)], xc, g, r, ALU.mult, ALU.mult))

    def out_proj(self, seq, t0, T, act, nk, W, scale, wslots, xsl, xo):
        S = self.S
        XS = seq["XS"].rearrange("p (c t) -> p c t", c=DC)
        def loads(ob):
            self.wget(wslots[ob % 2], W, ob, nk, 256)
            S.dma("sp", xsl[ob % 2], XS[:, 2 * ob:2 * ob + 2, t0:t0 + T])

        loads(0)
        for ob in range(8):
            if ob + 1 < 8:
                loads(ob + 1)
            S.flush()
            wb = wslots[ob % 2]
            xs_ = xsl[ob % 2]
            xo_ = xo[ob % 2]
            for oo in range(2):
                for s in range(T // 512):
                    b = self.bank()
                    S.mm(b, [(wb[:, kc, 128 * oo:128 * (oo + 1)], act[:, kc, 512 * s:512 * (s + 1)]) for kc in range(nk)])
                    S.stt(xo_[:, oo, 512 * s:512 * (s + 1)], b, float(scale), xs_[:, oo, 512 * s:512 * (s + 1)],
                          ALU.mult, ALU.add)
            S.dma("sp", XS[:, 2 * ob:2 * ob + 2, t0:t0 + T], xo_, defer=True)
        S.flush()

    def ffn(self, which, layer):
        S, A, T = self.S, self.A, self.T
        wi = self.I[f"ffn{which}_wi"][layer]
        wo = self.I[f"ffn{which}_wo"][layer]
        gcol = self.col("gains", 16, 16 * ((0 if which == 1 else 2) * self.depth + layer))
        self.prep(wi, DC, [[(256 * jb, 256), (DFF + 256 * jb, 256)] for jb in range(FC // 2)], self.WBI, 512)
        self.prep(wo, FC, [[(256 * ob, 256)] for ob in range(8)], self.WBO, 256)
        wo = self.WBO
        A.top = 0
        h = A.alloc(BF16, DC, T)
        act = A.alloc(BF16, FC, T)
        wtop = A.top
        wslots = [A.alloc(BF16, DC, 2, 256) for _ in range(3)]
        A.top = wtop
        woslots = [A.alloc(BF16, FC, 256) for _ in range(2)]
        A.top = max(A.top, wtop + 3 * 16384)
        sg = [A.alloc(F32, 512) for _ in range(2)]
        rt = A.alloc(F32, 512)
        top = A.top
        A.top = (DC * T * 2)
        xin = [A.alloc(F32, DC, 512) for _ in range(2 if T >= 1024 else 1)]
        sqb = A.alloc(BF16, DC, 512)
        assert T < 1024 or A.top <= (DC + FC) * T * 2
        A.top = 0
        xsl = [A.alloc(F32, 2, T) for _ in range(2)]
        xo = [A.alloc(F32, 2, T) for _ in range(2)]
        assert A.top <= DC * T * 2
        A.top = top
        for seq in self.seqs:
            for t0 in range(0, seq["L"], T):
                self.group_norm(seq, t0, T, gcol, h, (xin, sqb, rt))
                for jb in range(FC // 2):
                    wb = wslots[jb % 3]
                    self.wget(wb.rearrange("p k a j -> p k (a j)"), self.WBI, jb, DC, 512)
                    for s in range(T // 512):
                        for jj in range(2):
                            gb, ub = self.bank(), self.bank()
                            hs = [h[:, kc, 512 * s:512 * (s + 1)] for kc in range(DC)]
                            S.mm(gb, [(wb[:, kc, 0, 128 * jj:128 * (jj + 1)], hs[kc]) for kc in range(DC)])
                            S.mm(ub, [(wb[:, kc, 1, 128 * jj:128 * (jj + 1)], hs[kc]) for kc in range(DC)])
                            sgt = sg[jj]
                            S.act(sgt, gb, AF.Silu)
                            S.tt(act[:, 2 * jb + jj, 512 * s:512 * (s + 1)], sgt, ub, ALU.mult)
                self.out_proj(seq, t0, T, act, FC, wo, 0.5, woslots, xsl, xo)

    def build_bias(self):
        S, A, I = self.S, self.A, self.I
        A.top = 0
        rb = A.alloc(F32, 16)
        oh = A.alloc(F32, 512)
        mk = A.alloc(F32, 512)
        on1 = A.alloc(F32, 16)
        jr = A.alloc(F32, 128)
        tb = A.alloc(F32, 512)
        hk = A.alloc(F32, 384)
        bt = A.alloc(F32, 16, 384)
        S.dma("sp", rb[0:32, :], I["rel_bias"])
        S.dma("sp", oh[0:32, :], I["oh"])
        S.dma("sp", mk[0:1, :], I["mask"])
        S.dma("sp", jr, I["jrev"])
        S.memset(on1[0:1, :], 1.0)
        b = self.bank()
        S.mm(b[0:16, :], [(rb[0:32, :], oh[0:32, :]), (on1[0:1, :], mk[0:1, :])])
        S.copy(tb[0:16, :], b[0:16, :])
        S.dma("sp", self.TB, tb[0:16, :])
        for hh in range(16):
            src = bass.AP(tensor=self.TB.tensor, offset=self.TB.offset + 512 * hh, ap=[[1, 128], [1, 384]])
            S.dma("sp", hk, src)
            b = self.bank()
            S.mm(b[:, 0:384], [(jr, hk)])
            S.copy(bt[:, hh, :], b[:, 0:384])
        S.dma("sp", self.BIAS.rearrange("p (h k) -> p h k", h=16), bt)

    def mix_odd(self, i, layer):
        S, A, T = self.S, self.A, self.T
        if not getattr(self, "_bias_done", False):
            self.build_bias()
            self._bias_done = True
        wqkv = self.I["attn_w_qkv"][i]
        self.prep(wqkv, DC, [[(256 * b, 256)] for b in range(12)], self.WBM, 256)
        self.prep(self.I["attn_w_o"][i], DC, [[(256 * b, 256)] for b in range(8)], self.WBM2, 256)
        wo = self.WBM2
        gcol = self.col("gains", 16, 16 * (self.depth + layer))
        sink = self.col("sink", 16, 16 * i)
        for seq in self.seqs:
            L = seq["L"]
            nb = L // 128
            QT = self.QT.rearrange("p (c t) -> p c t", c=16)[:, :, 0:L] if False else sv(self.QT, 0, [[L, 16], [1, L]])
            KT = sv(self.KT, 0, [[L, 4], [1, L]])
            VT = sv(self.VT, 0, [[512, nb], [1, 512]])
            OT = sv(self.OT, 0, [[L, 16], [1, L]])
            A.top = 0
            h = A.alloc(BF16, DC, T)
            qk = A.alloc(BF16, 20, T)
            vst = A.alloc(BF16, T // 128, 512)
            wslots = [A.alloc(BF16, DC, 256) for _ in range(3)]
            wv = [A.alloc(BF16, DC, 256) for _ in range(2)]
            xin = A.alloc(F32, DC, 512)
            sqb = A.alloc(BF16, DC, 512)
            rt = A.alloc(F32, 512)
            for t0 in range(0, L, T):
                self.group_norm(seq, t0, T, gcol, h, (xin, sqb, rt))
                for cb in range(10):
                    wb = wslots[cb % 3]
                    self.wget(wb, self.WBM, cb, DC, 256)
                    for jj in range(2):
                        for s in range(T // 512):
                            b = self.bank()
                            S.mm(b, [(wb[:, kc, 128 * jj:128 * (jj + 1)], h[:, kc, 512 * s:512 * (s + 1)]) for kc in range(DC)])
                            S.copy(qk[:, 2 * cb + jj, 512 * s:512 * (s + 1)], b, e=("act" if s % 2 else "dve"))
                self.wget(wv[0], self.WBM, 10, DC, 256)
                self.wget(wv[1], self.WBM, 11, DC, 256)
                for tt in range(T // 128):
                    b = self.bank()
                    for hv in range(2):
                        S.mm(b[:, 256 * hv:256 * (hv + 1)], [(h[:, kc, 128 * tt:128 * (tt + 1)], wv[hv][:, kc, :]) for kc in range(DC)])
                    S.copy(vst[:, tt, :], b, e=("act" if tt % 2 else "dve"))
                S.dma("sp", QT[:, :, t0:t0 + T], qk[:, 0:16, :])
                S.dma("sp", KT[:, :, t0:t0 + T], qk[:, 16:20, :])
                S.dma("sp", VT[:, t0 // 128:(t0 + T) // 128, :], vst)
            A.top = 0
            bias = A.alloc(F32, 16, 384)
            S.dma("sp", bias, self.BIAS.rearrange("p (h k) -> p h k", h=16))
            ktg = A.alloc(BF16, L)
            vtg = A.alloc(BF16, nb, 128)
            QB = min(8, nb)
            qt = A.alloc(BF16, 4, 128 * QB)
            ost = A.alloc(BF16, 4, 128 * QB)
            sc = A.alloc(F32, 4, 384)
            pe_ = A.alloc(F32, 4, 384)
            pb = A.alloc(BF16, 4, 384)
            pT = A.alloc(BF16, 3, 512)
            st = A.alloc(F32, 32)
            isq = 1.0 / math.sqrt(HD)
            for g in range(NKV):
                S.dma("sp", ktg, KT[:, g, :])
                S.dma("sp", vtg, VT[:, :, 128 * g:128 * (g + 1)])
                sk4 = sink[:, 4 * g:4 * g + 4]
                for blk in range(nb):
                    bq = blk % QB
                    if bq == 0:
                        S.dma("sp", qt, QT[:, 4 * g:4 * g + 4, 128 * blk:128 * (blk + QB)])
                    kb0, kb1 = max(blk - 1, 0), min(blk + 1, nb - 1)
                    nkb = kb1 - kb0 + 1
                    W = 128 * nkb
                    boff = 128 * (kb0 - (blk - 1))
                    for h4 in range(4):
                        S.mm(self.bank(h4)[:, 0:W], [(qt[:, h4, 128 * bq:128 * (bq + 1)], ktg[:, 128 * kb0:128 * kb0 + W])])
                    psv = sv(self.ps, 0, [[512, 4], [1, W]])
                    S.stt(sc[:, :, 0:W], psv, isq, bias[:, 4 * g:4 * g + 4, boff:boff + W], ALU.mult, ALU.add)
                    rmax, negm, rsum, es, den, rinv, tmp = (st[:, 4 * k:4 * k + 4] for k in range(7))
                    S.reduce(rmax, sc[:, :, 0:W], ALU.max)
                    S.tt(negm, rmax, sk4, ALU.max)
                    S.ts(negm, negm, -1.0, None, ALU.mult)
                    S.memset(rsum, 0.0)
                    for h4 in range(4):
                        S.act(pe_[:, h4, 0:W], sc[:, h4, 0:W], AF.Exp, bias=negm[:, h4:h4 + 1],
                              accum_out=rsum[:, h4:h4 + 1])
                    S.tt(tmp, negm, sk4, ALU.add)
                    S.act(es, tmp, AF.Exp)
                    S.tt(den, rsum, es, ALU.add)
                    S.recip(rinv, den)
                    S.tt(pb[:, :, 0:W], pe_[:, :, 0:W], sv(rinv, 0, [[1, 4], [0, W]]), ALU.mult)
                    for kb in range(nkb):
                        bb = self.bankb(4 + kb)
                        S.tr([(bb[:, 128 * h4:128 * (h4 + 1)], pb[:, h4, 128 * kb:128 * (kb + 1)]) for h4 in range(4)],
                             self.identb)
                        S.copy(pT[:, kb, :], bb[:, 0:512], e=("act" if kb % 2 else "dve"))
                    ob_ = self.bank(7)
                    S.mm(ob_, [(vtg[:, kb0 + kb, :], pT[:, kb, :]) for kb in range(nkb)])
                    S.copy(sv(ost, 128 * bq, [[128 * QB, 4], [1, 128]]), sv(ob_, 0, [[128, 4], [1, 128]]), e="act")
                    if bq == QB - 1:
                        S.dma("sp", OT[:, 4 * g:4 * g + 4, 128 * (blk - QB + 1):128 * (blk + 1)], ost)
            A.top = 0
            o = A.alloc(BF16, DC, T)
            woslots = [A.alloc(BF16, DC, 256) for _ in range(2)]
            xsl = [A.alloc(F32, 2, T) for _ in range(2)]
            xo = [A.alloc(F32, 2, T) for _ in range(2)]
            for t0 in range(0, L, T):
                S.dma("sp", o, OT[:, :, t0:t0 + T])
                self.out_proj(seq, t0, T, o, DC, wo, 1.0, woslots, xsl, xo)

    def cmul(self, src_bank, tab, dst, k, tP, tQ):
        S = self.S
        a_all = sv(src_bank, 0, [[256, 2], [128, 2], [1, 128]])
        a_re = sv(src_bank, 0, [[256, 2], [1, 128]])
        a_im = sv(src_bank, 128, [[256, 2], [1, 128]])
        S.tt(sv(tP, 0, [[256, 2], [128, 2], [1, 128]]), a_all, sv(tab, 0, [[0, 2], [0, 2], [1, 128]]), ALU.mult)
        S.tt(sv(tQ, 0, [[256, 2], [1, 128]]), a_im, sv(tab, 256, [[0, 2], [1, 128]]), ALU.mult)
        S.tt(sv(tQ, 128, [[256, 2], [1, 128]]), a_re, sv(tab, 128, [[0, 2], [1, 128]]), ALU.mult)
        S.tt(sv(dst, 2 * k * 128, [[128, 2], [512, 2], [1, 128]]), sv(tP, 0, [[256, 2], [128, 2], [1, 128]]),
             sv(tQ, 0, [[256, 2], [128, 2], [1, 128]]), ALU.add)

    def fft_fwd(self, srcT, nK, N2, tb, bufs, sink):
        S = self.S
        Bt, tP, tQ = bufs
        for b in range(N2 // 4):
            for k in range(2):
                bk = self.bank()
                for g2 in range(2):
                    gi = 4 * b + 2 * k + g2
                    S.mm(bk[:, 256 * g2:256 * (g2 + 1)], [(srcT[0:nK, 128 * gi:128 * (gi + 1)], self.f1[0:nK, :])])
                self.cmul(bk, tb["tw1"], Bt, k, tP, tQ)
            bre = Bt[:, 0, :, :].rearrange("p g k -> p (g k)")
            bim = Bt[:, 1, :, :].rearrange("p g k -> p (g k)")
            xre, xim = self.bank(), self.bank()
            S.mm(xre, [(tb["bd"][:, 0, :], bre), (tb["bd"][:, 2, :], bim)])
            S.mm(xim, [(tb["bd"][:, 1, :], bre), (tb["bd"][:, 0, :], bim)])
            sink(b, xre, xim)

    def mix_even(self, i, layer):
        S, A, T, I = self.S, self.A, self.T, self.I
        self.prep(I["ab_w_in"][i], DC, [[(256 * b, 256)] for b in range(16)], self.WBM, 256)
        self.prep(I["ab_w_out"][i], DC, [[(256 * b, 256)] for b in range(8)], self.WBM2, 256)
        pw = I["pool_w"][i]
        pw2 = bass.AP(tensor=pw.tensor, offset=pw.offset, ap=[[256, 1024], [1, 256]])
        self.prep(pw2, 8, [[(0, 256)]], self.WBP, 256)
        gcol = self.col("gains", 16, 16 * (self.depth + layer))
        c_cw = self.cols["convw"][0] + 72 * i
        c_cb = self.cols["convb"][0] + 24 * i
        c_d = self.cols["hyd"][0] + 8 * i
        c_b4 = self.cols["b4"][0] + 16 * i
        c_mlp = self.cols["mlp"][0] + 8 * i
        c_nd = self.cols["ndelta"][0]
        c_ps = self.cols["pscale"][0] + 8 * i
        cv = self.cv
        for seq in self.seqs:
            L = seq["L"]
            tag = seq["name"]
            N2 = L // 64
            U3 = sv(self.U, 0, [[L, 32], [1, L]])
            PB3 = sv(self.PB, 0, [[L, 8], [1, L]])
            YB3 = sv(self.YB, 0, [[L, 8], [1, L]])
            A.top = 0
            h = A.alloc(BF16, DC, T)
            wslots = [A.alloc(BF16, DC, 256) for _ in range(3)]
            ust = [A.alloc(F32, 2, T) for _ in range(2)]
            xin = A.alloc(F32, DC, 512)
            sqb = A.alloc(BF16, DC, 512)
            rt = A.alloc(F32, 512)
            for t0 in range(0, L, T):
                self.group_norm(seq, t0, T, gcol, h, (xin, sqb, rt))
                self.wget(wslots[0], self.WBM, 0, DC, 256)
                for cb in range(16):
                    wb = wslots[cb % 3]
                    if cb + 1 < 16:
                        self.wget(wslots[(cb + 1) % 3], self.WBM, cb + 1, DC, 256)
                    S.flush()
                    us = ust[cb % 2]
                    for jj in range(2):
                        for s_ in range(T // 512):
                            b = self.bank()
                            S.mm(b, [(wb[:, kc, 128 * jj:128 * (jj + 1)], h[:, kc, 512 * s_:512 * (s_ + 1)]) for kc in range(DC)])
                            S.copy(us[:, jj, 512 * s_:512 * (s_ + 1)], b, e=("act" if s_ % 2 else "dve"))
                    S.dma("sp", U3[:, 2 * cb:2 * cb + 2, t0:t0 + T], us, defer=True)
                S.flush()
            A.top = 0
            ub = A.alloc(F32, L + 16)
            t1 = A.alloc(F32, L + 16)
            t2 = A.alloc(F32, L + 16)
            pout = A.alloc(BF16, L)
            ec = A.alloc(F32, 64)
            e8 = A.alloc(F32, 16)
            S.dma("sp", ec, I[f"ec_{tag}"])
            S.memset(ub[:, 0:8], 0.0)
            S.memset(ub[:, L + 8:L + 16], 0.0)

            def R(buf, lo, hi):
                return buf[:, 8 + lo:8 + hi]

            for c in range(8):
                gi = c // 2
                w = POOL_WINDOWS[gi]
                S.dma("sp", R(ub, 0, L), U3[:, c, :])
                steps = {2: [(0, -1, 0)], 4: [(1, -1, 0), (0, -1, 1)], 8: [(3, -1, 0), (2, -1, 1), (0, -2, 2)],
                         16: [(7, -1, 0), (6, -1, 1), (4, -2, 2), (0, -4, 4)]}[w]
                src = ub
                dsts = [t1, t2]
                for li, (m, sa, sb) in enumerate(steps):
                    dst = dsts[li % 2]
                    S.tt(R(dst, -m, L + m), R(src, -m + sa, L + m + sa), R(src, -m + sb, L + m + sb), ALU.add)
                    src = dst
                ssum = src
                pf = dsts[len(steps) % 2]
                S.stt(R(pf, 0, L), R(ssum, 0, L), 1.0 / w, R(ub, 0, L), ALU.mult, ALU.subtract)
                for (lo, eo) in ((0, 0), (L - 8, 8)):
                    S.tt(e8[:, eo:eo + 8], R(ssum, lo, lo + 8), ec[:, 16 * gi + eo:16 * gi + eo + 8], ALU.mult)
                    S.tt(R(pf, lo, lo + 8), e8[:, eo:eo + 8], R(ub, lo, lo + 8), ALU.subtract)
                S.copy(pout, R(pf, 0, L), e="act")
                S.dma("sp", PB3[:, c, :], pout)
            A.top = 0
            tb = {}
            tb["tw1"] = A.alloc(F32, 384)
            tb["tw2"] = A.alloc(F32, 384)
            tb["bd"] = A.alloc(BF16, 3, 128)
            tb["cinv"] = A.alloc(BF16, 2, 256)
            S.dma("sp", tb["tw1"], I[f"tw1_{tag}"].rearrange("p a k -> p (a k)"))
            S.dma("sp", tb["tw2"], I[f"tw2_{tag}"].rearrange("p a k -> p (a k)"))
            S.dma("sp", tb["bd"], I[f"bd_{tag}"])
            S.dma("sp", tb["cinv"], I[f"cinv_{tag}"])
            w1 = A.alloc(F32, 64)
            w2 = A.alloc(F32, 64)
            w3 = A.alloc(F32, 64)
            w4 = A.alloc(F32, 2048)
            S.dma("sp", w1[0:33, :], I["hy_ff_w1"][i])
            S.dma("sp", w2[0:64, :], I["hy_ff_w2"][i])
            S.dma("sp", w3[0:64, :], I["hy_ff_w3"][i])
            S.dma("sp", w4[0:64, :], I["hy_ff_w4"][i])
            mark = A.top
            zt = A.alloc(F32, 512)
            ta = A.alloc(F32, 512)
            tn = A.alloc(F32, 512)
            hb = [A.alloc(F32, 512) for _ in range(2)]
            zext = I[f"zext_{tag}"]
            text = I[f"text_{tag}"]
            H3 = self.H3
            for j0 in range(0, 2 * L, 512):
                S.dma("sp", zt[0:33, :], zext[:, j0:j0 + 512])
                rhs = zt[0:33, :]
                for li, (wl, kk) in enumerate(((w1, 33), (w2, 64), (w3, 64))):
                    b = self.bank()
                    S.mm(b[0:64, :], [(wl[0:kk, 0:64], rhs)])
                    S.ts(ta[0:64, :], b[0:64, :], cv[0:64, c_mlp + 3:c_mlp + 4], cv[0:64, c_mlp + 4 + li:c_mlp + 5 + li],
                         ALU.mult, ALU.add)
                    S.ts(tn[0:64, :], ta[0:64, :], 1.0 / (2.0 * math.pi), 12582912.0, ALU.mult, ALU.add)
                    S.ts(tn[0:64, :], tn[0:64, :], -12582912.0, -2.0 * math.pi, ALU.add, ALU.mult)
                    S.tt(ta[0:64, :], ta[0:64, :], tn[0:64, :], ALU.add)
                    S.ts(ta[0:64, :], ta[0:64, :], 3.1415925, -3.1415925, ALU.min, ALU.max)
                    ho = hb[li % 2]
                    S.act(ho[0:64, :], ta[0:64, :], AF.Sin)
                    rhs = ho[0:64, :]
                S.dma("sp", H3[:, j0:j0 + 512], rhs)
            A.top = mark
            ld = A.alloc(F32, L + 2)
            x0c = A.alloc(BF16, L)
            vx = A.alloc(F32, L)
            xT = A.alloc(BF16, 128 * N2)
            yT = bass.AP(tensor=xT.tensor, offset=vx.offset * 2, ap=[list(xT.ap[0]), [1, 128 * N2]])
            Bt = A.alloc(BF16, 2, 4, 128)
            Yt = A.alloc(BF16, 2, 4, 128)
            Gt = A.alloc(BF16, 2, 4, 128)
            tP = A.alloc(F32, 512)
            tQ = A.alloc(F32, 512)
            kft = [A.alloc(F32, 2, 4, 128) for _ in range(2)]
            tm = [A.alloc(F32, 512) for _ in range(4)]
            h3t = A.alloc(F32, 512)
            txt = A.alloc(F32, 512)
            dec = A.alloc(F32, 512)
            ksum = A.alloc(F32, 64)
            rn = A.alloc(F32, 4)
            cbuf = bass.AP(tensor=vx.tensor, offset=(xT.offset // 2), ap=[list(vx.ap[0]), [1, L]])
            kbuf = bass.AP(tensor=xT.tensor, offset=ld.offset * 2, ap=[list(xT.ap[0]), [1, 2 * L]])
            yb = xT[:, 0:L]
            S.memset(ld[:, 0:1], 0.0)
            S.memset(ld[:, L + 1:L + 2], 0.0)
            KFv = self.KF
            nt = (2 * L) // 512

            def conv3(j, out, first_out=None):
                S.memset(ld[:, 0:1], 0.0)
                S.dma("sp", ld[:, 1:L + 1], U3[:, 8 + j, :])
                acc = first_out if first_out is not None else out
                S.ts(acc, ld[:, 1:L + 1], cv[:, c_cw + 24 + j:c_cw + 25 + j], cv[:, c_cb + j:c_cb + j + 1], ALU.mult, ALU.add)
                S.stt(acc, ld[:, 0:L], cv[:, c_cw + j:c_cw + j + 1], acc, ALU.mult, ALU.add)
                S.stt(out, ld[:, 2:L + 2], cv[:, c_cw + 48 + j:c_cw + 49 + j], acc, ALU.mult, ALU.add)

            for c in range(8):
                for ti in range(nt):
                    j0 = 512 * ti
                    S.dma("sp", h3t[0:64, :], H3[:, j0:j0 + 512])
                    S.dma("sp", txt, bass.AP(tensor=text.tensor, offset=text.offset + j0, ap=[[0, 128], [1, 512]]))
                    S.act(dec, txt, AF.Exp, scale=cv[:, c_nd + c:c_nd + c + 1])
                    back = j0 >= L
                    wc = 1024 * back + 128 * c
                    b = self.bank()
                    S.mm(b, [(w4[0:64, wc:wc + 128], h3t[0:64, :])])
                    kf32 = tm[ti % 2]
                    S.stt(kf32, b, cv[:, c_b4 + 8 * back + c:c_b4 + 8 * back + c + 1], dec, ALU.add, ALU.mult)
                    if j0 == L:
                        S.memset(kf32[:, 0:1], 0.0)
                    S.reduce(ksum[:, ti:ti + 1], kf32, ALU.add, absval=True)
                    S.copy(kbuf[:, j0:j0 + 512], kf32, e="act")
                S.reduce(rn[:, 0:1], ksum[:, 0:nt], ALU.add)
                S.recip(rn[:, 1:2], rn[:, 0:1])
                for q4 in range(N2 // 4):
                    bb = self.bankb(self.bankrr)
                    self.bankrr = (self.bankrr + 1) % 8
                    S.tr([(bb[:, 128 * q:128 * (q + 1)], sv(kbuf, 4 * q4 + q, [[N2, 128]])) for q in range(4)], self.identb)
                    S.copy(sv(xT, 4 * q4, [[N2, 128], [1, 4]]), sv(bb, 0, [[1, 128], [128, 4]]), e=("act" if q4 % 2 else "dve"))

                def fsink(b, xre, xim):
                    kt = kft[b % 2]
                    S.copy(kt[:, 0, :, :].rearrange("p g k -> p (g k)"), xre, e="act")
                    S.copy(kt[:, 1, :, :].rearrange("p g k -> p (g k)"), xim, e="dve")
                    S.dma("sp", bass.AP(tensor=KFv.tensor, offset=KFv.offset + 1024 * b, ap=[[N2 * 256, 128], [1, 1024]]),
                          kt.rearrange("p a g k -> p (a g k)"))

                self.fft_fwd(xT, 128, N2, tb, (Bt, tP, tQ), fsink)
                conv3(16 + c, vx)
                conv3(8 + c, cbuf)
                S.tt(vx, vx, cbuf, ALU.mult)
                conv3(c, x0c, first_out=cbuf)
                S.ts(ld[:, 1:L + 1], vx, cv[:, c_d + c:c_d + c + 1], 0.0, ALU.mult, ALU.add)
                z1 = ld[:, 1:L + 1]
                for q4 in range(N2 // 4):
                    bk = self.bank()
                    S.tr([(bk[0:64, 128 * q:128 * (q + 1)], sv(vx, 4 * q4 + q, [[N2, 64]])) for q in range(4)], self.ident)
                    S.copy(sv(xT[0:64, :], 4 * q4, [[N2, 128], [1, 4]]), sv(bk[0:64, :], 0, [[1, 128], [128, 4]]),
                           e=("act" if q4 % 2 else "dve"))

                def dsink(b, xre, xim):
                    kt = kft[b % 2]
                    S.dma("sp", kt.rearrange("p a g k -> p (a g k)"),
                          bass.AP(tensor=KFv.tensor, offset=KFv.offset + 1024 * b, ap=[[N2 * 256, 128], [1, 1024]]))
                    kre = kt[:, 0, :, :].rearrange("p g k -> p (g k)")
                    kim = kt[:, 1, :, :].rearrange("p g k -> p (g k)")
                    S.tt(tm[0], xre, kre, ALU.mult)
                    S.tt(tm[1], xim, kim, ALU.mult)
                    S.tt(Yt[:, 0, :, :].rearrange("p g k -> p (g k)"), tm[0], tm[1], ALU.subtract)
                    S.tt(tm[2], xre, kim, ALU.mult)
                    S.tt(tm[3], xim, kre, ALU.mult)
                    S.tt(Yt[:, 1, :, :].rearrange("p g k -> p (g k)"), tm[2], tm[3], ALU.add)
                    for k in range(2):
                        bk = self.bank()
                        for g2 in range(2):
                            gi = 2 * k + g2
                            S.mm(bk[:, 256 * g2:256 * (g2 + 1)], [(Yt[:, 0, gi, :], tb["cinv"][:, 0, :]),
                                                                  (Yt[:, 1, gi, :], tb["cinv"][:, 1, :])])
                        self.cmul(bk, tb["tw2"], Gt, k, tP, tQ)
                    be = self.bank()
                    S.mm(be[0:64, :], [(self.etab[:, 0, :], Gt[:, 0, :, :].rearrange("p g k -> p (g k)")),
                                       (self.etab[:, 1, :], Gt[:, 1, :, :].rearrange("p g k -> p (g k)"))])
                    S.copy(yT[0:64, 512 * b:512 * (b + 1)], be[0:64, :], e="act")

                self.fft_fwd(xT, 64, N2, tb, (Bt, tP, tQ), dsink)
                for j in range(N2 // 8):
                    bb = self.bankb(self.bankrr)
                    self.bankrr = (self.bankrr + 1) % 8
                    S.tr([(bb[:, 64 * q:64 * (q + 1)], sv(yT[0:64, :], 8 * j + q, [[N2, 128]])) for q in range(8)], self.identb)
                    tf = tm[j % 2]
                    S.stt(sv(tf, 0, [[64, 8], [1, 64]]), sv(bb, 0, [[64, 8], [1, 64]]), rn[:, 1:2],
                          sv(z1, 8 * j, [[1, 8], [N2, 64]]), ALU.mult, ALU.add)
                    S.tt(sv(yb, 8 * j, [[1, 8], [N2, 64]]), sv(tf, 0, [[64, 8], [1, 64]]),
                         sv(x0c, 8 * j, [[1, 8], [N2, 64]]), ALU.mult)
                S.dma("sp", YB3[:, c, :], yb)
            A.top = 0
            cat = A.alloc(BF16, DC, T)
            pin = A.alloc(BF16, 8, T)
            wp = A.alloc(BF16, 8, 256)
            woslots = [A.alloc(BF16, DC, 256) for _ in range(2)]
            xsl = [A.alloc(F32, 2, T) for _ in range(2)]
            xo = [A.alloc(F32, 2, T) for _ in range(2)]
            self.wget(wp, self.WBP, 0, 8, 256)
            for t0 in range(0, L, T):
                S.dma("sp", pin, PB3[:, :, t0:t0 + T])
                S.dma("sp", cat[:, 8:16, :], YB3[:, :, t0:t0 + T])
                for g in range(4):
                    for oc in range(2):
                        for s_ in range(T // 512):
                            b = self.bank()
                            S.mm(b, [(wp[:, 2 * g + kc, 128 * oc:128 * (oc + 1)], pin[:, 2 * g + kc, 512 * s_:512 * (s_ + 1)])
                                     for kc in range(2)])
                            cc = c_ps + 2 * g + oc
                            S.ts(cat[:, 2 * g + oc, 512 * s_:512 * (s_ + 1)], b, cv[:, cc:cc + 1], 0.0, ALU.mult, ALU.add)
                self.out_proj(seq, t0, T, cat, DC, self.WBM2, 1.0, woslots, xsl, xo)


def pack_cvec(inp, depth):
    n_even, n_odd = (depth + 1) // 2, depth // 2
    cols = {}
    parts = []
    pos = [0]

    def add(name, arr):
        arr = np.ascontiguousarray(arr, dtype=np.float32)
        cols[name] = (pos[0], arr.shape[1])
        parts.append(arr)
        pos[0] += arr.shape[1]

    g = [fm(inp["norm_ffn1"][l], 16) for l in range(depth)] + [fm(inp["norm_mix"][l], 16) for l in range(depth)] \
        + [fm(inp["norm_ffn2"][l], 16) for l in range(depth)] + [fm(inp["norm_final"], 16)]
    add("gains", np.concatenate(g, 1))
    z = np.zeros((128, 1), np.float32)
    if n_even:
        add("pscale", np.concatenate([fm(inp["pool_scale"][i], 8) for i in range(n_even)], 1))
        add("convw", np.concatenate([fm(inp["hy_conv_w"][i][k], 24) for i in range(n_even) for k in range(3)], 1))
        add("convb", np.concatenate([fm(inp["hy_conv_b"][i], 24) for i in range(n_even)], 1))
        add("hyd", np.concatenate([fm(inp["hy_d"][i], 8) for i in range(n_even)], 1))
        add("b4", np.concatenate([fm(inp["hy_ff_b4"][i], 16) for i in range(n_even)], 1))
        m = np.zeros((128, 8 * n_even), np.float32)
        for i in range(n_even):
            for k, nm in enumerate(("hy_ff_b1", "hy_ff_b2", "hy_ff_b3", "hy_freq")):
                m[0:64, 8 * i + k] = inp[nm][i]
        add("mlp", m)
    else:
        add("mlp", np.zeros((128, 8), np.float32))
    if n_odd:
        add("sink", np.concatenate([np.broadcast_to(np.asarray(inp["attn_sink"][i], np.float32)[None, :], (128, 16))
                                    for i in range(n_odd)], 1))
    mind = math.log(1e-2) / 1.5
    maxd = math.log(1e-2) / 0.3
    deltas = np.linspace(mind, maxd, CH, dtype=np.float32)
    add("ndelta", -np.abs(fm(deltas, 8)))
    return np.concatenate(parts, 1), cols


_CACHE = {}


def kernel(**inputs):
    cfg = dict(CFG)
    cfg.update(inputs.pop("_cfg", {}))
    inp = {k: np.asarray(v) for k, v in inputs.items()}
    depth = cfg["depth"]
    Lp, Ls = cfg["Lp"], cfg["Ls"]
    cvec, cols = pack_cvec(inp, depth)
    ct = common_tables()
    host = dict(cvec=cvec, ident=ct["ident"], identb=ct["identb"], f1=ct["f1"], etab=ct["e"], oh=ct["oh"],
                mask=ct["mask"], jrev=ct["jrev"])
    for tag, L in (("p", Lp), ("s", Ls)):
        ft = fft_tables(L)
        for k in ("tw1", "tw2", "bd", "cinv", "zext", "text"):
            host[f"{k}_{tag}"] = ft[k]
        host[f"ec_{tag}"] = pool_edges(L)
    for k in ("ffn1_wi", "ffn1_wo", "ffn2_wi", "ffn2_wo", "ab_w_in", "ab_w_out", "pool_w", "attn_w_qkv", "attn_w_o",
              "rel_bias", "hy_ff_w1", "hy_ff_w2", "hy_ff_w3", "hy_ff_w4"):
        plan = cfg.get("plan")
        if plan is not None and k.startswith("ffn") and not any(p[0] == "ffn" and f"ffn{p[1]}" == k[:4] for p in plan):
            continue
        if inp[k].size > 0:
            host[k] = np.ascontiguousarray(inp[k], dtype=np.float32)
    xp = np.ascontiguousarray(inp["x_prompt"][0], dtype=np.float32)
    xs = np.asarray(inp["x_sample"], dtype=np.float32)
    nb = xs.shape[0]
    host["xp"] = xp
    host["xs"] = np.ascontiguousarray(xs[0])
    import time as _time
    _t0 = _time.time()
    nc = bass.Bass("TRN2", target_bir_lowering=False)
    prog = Prog(cfg, cols, cvec.shape[1])
    prog.build(nc, host)
    print(f"[kernel] build {_time.time() - _t0:.1f}s ops={prog.S.nops} sems={prog.S.nsem}", flush=True)
    _t0 = _time.time()
    if cfg.get("build_only"):
        return None
    in_maps = []
    for c in range(8):
        m = dict(host)
        m["xs"] = np.ascontiguousarray(xs[c % nb])
        in_maps.append(m)
    res = run_bass_kernel_spmd(nc, in_maps, core_ids=list(range(8)))
    print(f"[kernel] run {_time.time() - _t0:.1f}s", flush=True)
    yp = np.asarray(res.results[0]["yp"], dtype=np.float32)[None]
    ys = np.stack([np.asarray(res.results[c]["ys"], dtype=np.float32) for c in range(nb)], 0)
    return (yp, ys)
```

```python
import math
from contextlib import ExitStack

import numpy as np
import ml_dtypes

import concourse.bass as bass
import concourse.mybir as mybir
from concourse.bass_utils import run_bass_kernel_spmd

F32 = mybir.dt.float32
BF16 = mybir.dt.bfloat16
AF = mybir.ActivationFunctionType
ALU = mybir.AluOpType
AX = mybir.AxisListType

D = 2048
DC = 16
DFF = 5632
FC = 44
NH = 16
NKV = 4
HD = 128
EPS = 1e-6
POOL_WINDOWS = (2, 4, 8, 16)
CH = 1024
NEG = -30000.0

CFG = dict(Lp=8192, Ls=2048, depth=4, T=1024, plan=None)


def _esize(dt):
    return 4 if dt == F32 else 2


class Sched:
    SEM_LIMIT = 24000

    def __init__(self, nc, stack):
        self.nc = nc
        self.stack = stack
        self.eng = {"pe": nc.tensor, "act": nc.scalar, "dve": nc.vector, "pool": nc.gpsimd, "sp": nc.sync}
        self.nsem = 0
        self.csem = {}
        self.observed = {e: {} for e in self.eng}
        self.records = {}
        self.dma_slots = {}
        self.dma_rr = {}
        for e in ("sp", "pool", "act"):
            self.dma_slots[e] = [[self.new_sem(), 0] for _ in range(8)]
            self.dma_rr[e] = 0
        self.nops = 0
        self.deferred = []

    def new_sem(self):
        self.nsem += 1
        return self.stack.enter_context(self.nc.semaphore(f"s{self.nsem}"))

    @staticmethod
    def rng(ap):
        t = ap.tensor
        es = _esize(ap.dtype)
        pairs = ap.ap
        off = ap.offset
        kind = type(t).__name__
        if "DRam" in kind:
            lo = off
            hi = off
            for st, cnt in pairs:
                if st >= 0:
                    hi += st * (cnt - 1)
                else:
                    lo += st * (cnt - 1)
            return (t.name, lo * es, (hi + 1) * es)
        rowlen = pairs[0][0]
        f = off % rowlen if rowlen > 0 else off
        lo = f
        hi = f
        for st, cnt in pairs[1:]:
            if st >= 0:
                hi += st * (cnt - 1)
            else:
                lo += st * (cnt - 1)
        return (t.name, lo * es, (hi + 1) * es)

    def _wait(self, e, sem, val):
        key = id(sem)
        ob = self.observed[e]
        if ob.get(key, 0) >= val:
            return
        self.eng[e].wait_ge(sem, val)
        ob[key] = val

    def op(self, e, fn, reads=(), writes=(), dma=False, sig=True):
        self.nops += 1
        rr = [self.rng(a) for a in reads]
        ww = [self.rng(a) for a in writes]
        deps = {}
        for sp, lo, hi in rr:
            for rec in self.records.get(sp, ()):
                if rec[4] and rec[0] < hi and lo < rec[1]:
                    k = id(rec[2])
                    if deps.get(k, (None, 0))[1] < rec[3]:
                        deps[k] = (rec[2], rec[3])
        for sp, lo, hi in ww:
            for rec in self.records.get(sp, ()):
                if rec[0] < hi and lo < rec[1]:
                    k = id(rec[2])
                    if deps.get(k, (None, 0))[1] < rec[3]:
                        deps[k] = (rec[2], rec[3])
        if dma:
            slots = self.dma_slots[e]
            si = self.dma_rr[e]
            self.dma_rr[e] = (si + 1) % len(slots)
            slot = slots[si]
            if slot[1] >= self.SEM_LIMIT:
                slot[0] = self.new_sem()
                slot[1] = 0
            elif slot[1] > 0:
                k = id(slot[0])
                if deps.get(k, (None, 0))[1] < slot[1]:
                    deps[k] = (slot[0], slot[1])
        for sem, val in deps.values():
            self._wait(e, sem, val)
        ins = fn()
        if dma:
            slot[1] += 16
            ins.then_inc(slot[0], 16)
            ev = (slot[0], slot[1])
        else:
            cs = self.csem.get(e)
            if cs is None or cs[1] >= self.SEM_LIMIT:
                cs = [self.new_sem(), 0]
                self.csem[e] = cs
            cs[1] += 1
            ins.then_inc(cs[0], 1)
            ev = (cs[0], cs[1])
        for sp, lo, hi in ww:
            lst = self.records.setdefault(sp, [])
            lst[:] = [r for r in lst if not (lo <= r[0] and r[1] <= hi)]
            lst.append([lo, hi, ev[0], ev[1], True])
        for sp, lo, hi in rr:
            lst = self.records.setdefault(sp, [])
            lst[:] = [r for r in lst if not ((not r[4]) and r[2] is ev[0] and lo <= r[0] and r[1] <= hi)]
            lst.append([lo, hi, ev[0], ev[1], False])
        return ins

    def mm(self, out, pairs):
        n = len(pairs)
        reads = [a for p in pairs for a in p]

        def fn():
            ins = None
            for i, (l, r) in enumerate(pairs):
                ins = self.nc.tensor.matmul(out, l, r, start=(i == 0), stop=(i == n - 1))
            return ins

        return self.op("pe", fn, reads=reads, writes=[out])

    def tr(self, items, ident):
        reads = [i for _, i in items] + [ident]
        writes = [o for o, _ in items]

        def fn():
            ins = None
            for o, i in items:
                k = i.shape[0]
                ins = self.nc.tensor.transpose(o, i, ident[0:k, 0:k])
            return ins

        return self.op("pe", fn, reads=reads, writes=writes)

    def dma(self, e, out, in_, defer=False):
        if defer:
            self.deferred.append((e, out, in_))
            return None
        return self.op(e, lambda: self.eng[e].dma_start(out=out, in_=in_), reads=[in_], writes=[out], dma=True)

    def flush(self):
        d, self.deferred = self.deferred, []
        for e, out, in_ in d:
            self.dma(e, out, in_)

    def act(self, out, in_, func, bias=None, scale=None, accum_out=None, e="act"):
        reads = [in_]
        kw = {}
        if bias is not None:
            kw["bias"] = bias
            if not isinstance(bias, (int, float)):
                reads.append(bias)
        if scale is not None:
            kw["scale"] = scale
            if not isinstance(scale, (int, float)):
                reads.append(scale)
        writes = [out]
        if accum_out is not None:
            kw["accum_out"] = accum_out
            writes.append(accum_out)
        return self.op("act", lambda: self.nc.scalar.activation(out, in_, func, **kw), reads=reads, writes=writes)

    def tt(self, out, in0, in1, op, e="dve"):
        return self.op(e, lambda: self.eng[e].tensor_tensor(out, in0, in1, op), reads=[in0, in1], writes=[out])

    def ts(self, out, in0, s1, s2, op0, op1=None, e="dve"):
        reads = [in0] + [s for s in (s1, s2) if s is not None and not isinstance(s, (int, float))]
        if op1 is None:
            return self.op(e, lambda: self.eng[e].tensor_scalar(out, in0, s1, None, op0), reads=reads, writes=[out])
        return self.op(e, lambda: self.eng[e].tensor_scalar(out, in0, s1, s2, op0, op1), reads=reads, writes=[out])

    def stt(self, out, in0, scalar, in1, op0, op1, e="dve"):
        reads = [in0, in1] + ([] if isinstance(scalar, (int, float)) else [scalar])
        return self.op(e, lambda: self.eng[e].scalar_tensor_tensor(out, in0, scalar, in1, op0, op1),
                       reads=reads, writes=[out])

    def copy(self, out, in_, e="dve"):
        if e == "act":
            return self.op("act", lambda: self.nc.scalar.copy(out, in_), reads=[in_], writes=[out])
        return self.op(e, lambda: self.eng[e].tensor_copy(out, in_), reads=[in_], writes=[out])

    def memset(self, out, val, e="dve"):
        return self.op(e, lambda: self.eng[e].memset(out, val), reads=[], writes=[out])

    def reduce(self, out, in_, op, absval=False, e="dve"):
        kw = {"apply_absolute_value": True} if absval else {}
        return self.op(e, lambda: self.eng[e].tensor_reduce(out, in_, AX.X, op, **kw), reads=[in_], writes=[out])

    def recip(self, out, in_):
        return self.op("dve", lambda: self.nc.vector.reciprocal(out, in_), reads=[in_], writes=[out])

    def drain(self):
        self.flush()
        for e, cs in self.csem.items():
            self._wait("sp", cs[0], cs[1])
        for e, slots in self.dma_slots.items():
            for sem, val in slots:
                if val > 0:
                    self._wait("sp", sem, val)


def sv(ap, off, dims):
    return bass.AP(tensor=ap.tensor, offset=ap.offset + off, ap=[list(ap.ap[0])] + [list(d) for d in dims])


def fft_tables(L):
    N = 2 * L
    N2 = N // 128
    G = 128 // N2
    n2 = np.arange(N2)
    k1 = np.arange(128)
    ang = 2 * np.pi * np.outer(n2, k1) / N
    c = np.tile(np.cos(ang), (G, 1))
    d = np.tile(-np.sin(ang), (G, 1))
    tw1 = np.stack([c, d, -d], axis=1).astype(np.float32)
    ang2 = 2 * np.pi * np.outer(k1, n2) / N
    c2 = np.tile(np.cos(ang2), (1, G)) / N
    d2 = np.tile(np.sin(ang2), (1, G)) / N
    tw2 = np.stack([c2, d2, -d2], axis=1).astype(np.float32)
    a = 2 * np.pi * np.outer(n2, n2) / N2
    bre = np.kron(np.eye(G), np.cos(a))
    bim = np.kron(np.eye(G), -np.sin(a))
    bd = np.stack([bre, bim, -bim], axis=1).astype(ml_dtypes.bfloat16)
    cre = np.kron(np.eye(G), np.cos(a))
    cim = np.kron(np.eye(G), np.sin(a))
    cinv = np.stack([np.concatenate([cre, cim], 1), np.concatenate([-cim, cre], 1)], axis=1)
    cinv = cinv.astype(ml_dtypes.bfloat16)
    f32 = np.float32
    t = np.linspace(0.0, 1.0, L, dtype=f32)
    w = (2.0 * math.pi * np.arange(L, dtype=f32) / L).astype(f32)
    f = np.linspace(1e-4, 15, 16, dtype=f32)
    fw = (f[None, :] * w[:, None]).astype(f32)
    z = np.concatenate([t[:, None], np.cos(fw), -np.sin(fw)], axis=-1).astype(f32)
    pos = np.concatenate([np.arange(L), [0], L - np.arange(1, L)])
    zext = np.ascontiguousarray(z[pos].T).astype(f32)
    text = t[pos].astype(f32)
    return dict(N2=N2, G=G, tw1=tw1, tw2=tw2, bd=bd, cinv=cinv, zext=zext, text=text.reshape(1, -1))


def common_tables():
    n1 = np.arange(128)
    a = 2 * np.pi * np.outer(n1, n1) / 128
    f1 = np.concatenate([np.cos(a), -np.sin(a)], axis=1).astype(ml_dtypes.bfloat16)
    e = np.stack([np.cos(a)[:, :64], -np.sin(a)[:, :64]], axis=1).astype(ml_dtypes.bfloat16)
    ident = np.eye(128, dtype=np.float32)
    jrev = np.ascontiguousarray(np.eye(128, dtype=np.float32)[::-1])
    rel = np.arange(-255, 257)
    nb = 16
    me = 8
    n = np.abs(rel)
    large = me + (np.log(np.maximum(n, 1).astype(np.float32) / me) / math.log(128 / me) * (nb - me)).astype(np.int32)
    large = np.minimum(large, nb - 1)
    bucket = (rel > 0).astype(np.int32) * nb + np.where(n < me, n, large)
    valid = n <= 128
    oh = np.zeros((32, 512), np.float32)
    oh[bucket[valid], np.nonzero(valid)[0]] = 1.0
    mask = np.where(valid, 0.0, NEG).astype(np.float32).reshape(1, 512)
    return dict(f1=f1, e=e, ident=ident, identb=ident.astype(ml_dtypes.bfloat16), jrev=jrev, oh=oh, mask=mask)


def pool_edges(L):
    ec = np.zeros((4, 16), np.float32)
    for g, w in enumerate(POOL_WINDOWS):
        t = np.concatenate([np.arange(8), np.arange(L - 8, L)])
        lo = np.clip(t - w // 2, 0, L)
        hi = np.clip(t + w // 2, 0, L)
        ec[g] = 1.0 / (hi - lo)
    return np.ascontiguousarray(np.broadcast_to(ec.reshape(1, 64), (128, 64)))


def fm(v, nchunk):
    return np.ascontiguousarray(np.asarray(v, np.float32).reshape(nchunk, 128).T)


class Arena:
    def __init__(self, ap, nbytes):
        self.ap = ap
        self.nbytes = nbytes
        self.top = 0

    def alloc(self, dt, *shape):
        n = 1
        for s in shape:
            n *= s
        nb = n * _esize(dt)
        nb = (nb + 63) // 64 * 64
        off = self.top
        self.top += nb
        assert self.top <= self.nbytes, f"SBUF arena overflow {self.top} > {self.nbytes}"
        v = self.ap[:, off // 4:(off + nb) // 4]
        if dt != F32:
            v = v.bitcast(dt)
        v = v[:, 0:n]
        if len(shape) == 2:
            v = v.rearrange("p (a b) -> p a b", a=shape[0])
        elif len(shape) == 3:
            v = v.rearrange("p (a b c) -> p a b c", a=shape[0], b=shape[1])
        return v


class Prog:
    def __init__(self, cfg, cols, ncol):
        self.cfg = cfg
        self.cols = cols
        self.ncol = ncol
        self.Lp, self.Ls, self.depth, self.T = cfg["Lp"], cfg["Ls"], cfg["depth"], cfg["T"]
        self.n_even = (self.depth + 1) // 2
        self.n_odd = self.depth // 2

    def declare(self, nc, host):
        self.nc = nc
        self.I = {}
        for name, arr in host.items():
            dt = F32 if arr.dtype == np.float32 else BF16
            self.I[name] = nc.dram_tensor(name, list(arr.shape), dt, kind="ExternalInput").ap()
        Lp, Ls = self.Lp, self.Ls
        Lm = max(Lp, Ls)
        self.O = {
            "yp": nc.dram_tensor("yp", [Lp, D], F32, kind="ExternalOutput").ap(),
            "ys": nc.dram_tensor("ys", [Ls, D], F32, kind="ExternalOutput").ap(),
        }

        def scr(name, shape, dt):
            return nc.dram_tensor(name, shape, dt, kind="Internal").ap()

        self.seqs = [
            dict(name="p", x=self.I["xp"], y=self.O["yp"], L=Lp, XS=scr("XSp", [128, DC * Lp], F32)),
            dict(name="s", x=self.I["xs"], y=self.O["ys"], L=Ls, XS=scr("XSs", [128, DC * Ls], F32)),
        ]
        self.U = scr("U", [128, 32 * Lm], F32)
        self.PB = scr("PB", [128, 8 * Lm], BF16)
        self.YB = scr("YB", [128, 8 * Lm], BF16)
        self.KF = scr("KF", [128, (Lm // 64) * 256], F32)
        self.QT = scr("QT", [128, 16 * Lm], BF16)
        self.KT = scr("KT", [128, 4 * Lm], BF16)
        self.VT = scr("VT", [128, (Lm // 128) * 512], BF16)
        self.OT = scr("OT", [128, 16 * Lm], BF16)
        self.TB = scr("TB", [16, 512], F32)
        self.WBI = scr("WBI", [128, (FC // 2) * DC * 512], BF16)
        self.WBO = scr("WBO", [128, 8 * FC * 256], BF16)
        self.WBI2 = scr("WBI2", [128, (FC // 2) * DC * 512], BF16)
        self.WBO2 = scr("WBO2", [128, 8 * FC * 256], BF16)
        self.wsets = [(self.WBI, self.WBO), (self.WBI2, self.WBO2)]
        self.WBM = scr("WBM", [128, 16 * DC * 256], BF16)
        self.WBM2 = scr("WBM2", [128, 8 * DC * 256], BF16)
        self.WBP = scr("WBP", [128, 8 * 256], BF16)
        self.H3 = scr("H3", [64, 2 * Lm], F32)
        self.BIAS = scr("BIAS", [128, 16 * 384], F32)

    def build(self, nc, host):
        self.declare(nc, host)
        with ExitStack() as stack:
            SBW = 45056
            arena_t = stack.enter_context(nc.sbuf_tensor("arena", [128, SBW], F32))
            const_t = stack.enter_context(nc.sbuf_tensor("consts", [128, 3072], F32))
            ps_t = stack.enter_context(nc.psum_tensor("ps", [128, 4096], F32))
            self.S = Sched(nc, stack)
            self.A = Arena(arena_t[:, :], SBW * 4)
            self.CA = Arena(const_t[:, :], 3072 * 4)
            self.ps = ps_t[:, :]
            self.psb = ps_t[:, :].bitcast(BF16)
            self.bankrr = 0
            self.load_consts()
            for seq in self.seqs:
                self.convert_in(seq)
            plan = self.cfg.get("plan")
            if plan is None:
                plan = []
                for l in range(self.depth):
                    plan += [("ffn", 1, l), ("mix", l), ("ffn", 2, l)]
            self.bg = None
            self.bg_done = set()
            for pi, st in enumerate(plan):
                if st[0] == "ffn":
                    self.ffn(st[1], st[2])
                    continue
                nxt = []
                for st2 in plan[pi + 1:]:
                    if st2[0] != "ffn" or len(nxt) == 2 or any(st2[1] == q[1] for q in nxt):
                        break
                    nxt.append(st2)
                if self.cfg.get("bg", True) and nxt:
                    def chain(lst=nxt):
                        for q in lst:
                            yield from self.ffn_prep_gens(q[1], q[2])
                            self.bg_done.add((q[1], q[2]))
                    self.bg = chain()
                    self.bg_for = [(q[1], q[2]) for q in nxt]
                else:
                    self.bg_for = []
                if st[1] % 2 == 0:
                    self.mix_even(st[1] // 2, st[1])
                else:
                    self.mix_odd(st[1] // 2, st[1])
            for seq in self.seqs:
                self.final_out(seq)
            self.S.drain()
        return nc

    def bank(self, i=None):
        if i is None:
            i = self.bankrr
            self.bankrr = (self.bankrr + 1) % 8
        return self.ps[:, 512 * i:512 * (i + 1)]

    def bankb(self, i):
        return self.psb[:, 1024 * i:1024 * (i + 1)]

    def col(self, name, n=None, i=0):
        c0, w = self.cols[name]
        if n is None:
            return self.cv[:, c0:c0 + w]
        return self.cv[:, c0 + i:c0 + i + n]

    def load_consts(self):
        S, CA, I = self.S, self.CA, self.I
        self.cv = CA.alloc(F32, self.ncol)
        S.dma("sp", self.cv, I["cvec"])
        self.ident = CA.alloc(F32, 128)
        S.dma("sp", self.ident, I["ident"])
        self.identb = CA.alloc(BF16, 128)
        S.dma("sp", self.identb, I["identb"])
        self.ones = CA.alloc(BF16, 128)
        S.memset(self.ones, 1.0)
        self.f1 = CA.alloc(BF16, 256)
        S.dma("sp", self.f1, I["f1"])
        self.etab = CA.alloc(BF16, 2, 64)
        S.dma("sp", self.etab, I["etab"])
        g0, gw = self.cols["gains"]
        S.ts(self.cv[:, g0:g0 + gw], self.cv[:, g0:g0 + gw], math.sqrt(D), 0.0, ALU.mult, ALU.add)
        self.epsc = CA.alloc(F32, 4)
        S.memset(self.epsc, float(D * EPS))
        for i in range(self.n_even):
            for k in range(3):
                c0 = self.cols["mlp"][0] + 8 * i
                S.ts(self.cv[0:64, c0 + 4 + k:c0 + 5 + k], self.cv[0:64, c0 + k:c0 + k + 1],
                     self.cv[0:64, c0 + 3:c0 + 4], 0.0, ALU.mult, ALU.add)

    def convert_in(self, seq):
        S, A = self.S, self.A
        L = seq["L"]
        XS = seq["XS"].rearrange("p (c t) -> p c t", c=DC)
        A.top = 0
        xin = A.alloc(F32, 4, D)
        xo = A.alloc(F32, DC, 512)
        for t0 in range(0, L, 512):
            for k in range(4):
                S.dma("sp", xin[:, k, :], seq["x"][t0 + 128 * k:t0 + 128 * (k + 1), :])
            for c in range(DC):
                b = self.bank()
                S.tr([(b[:, 128 * k:128 * (k + 1)], xin[:, k, 128 * c:128 * (c + 1)]) for k in range(4)], self.ident)
                S.copy(xo[:, c, :], b, e=("act" if c % 2 else "dve"))
            S.dma("sp", XS[:, :, t0:t0 + 512], xo)

    def norm_tile(self, XSv, t0, gcol, xin, sqb, rt, out_fn):
        S = self.S
        S.dma("sp", xin, XSv[:, :, t0:t0 + 512])
        S.act(sqb, xin, AF.Square)
        b = self.bank()
        S.mm(b, [(self.ones, sqb[:, c, :]) for c in range(DC)])
        S.act(rt, b, AF.Ln, bias=self.epsc[:, 0:1])
        S.act(rt, rt, AF.Exp, scale=-0.5)
        for c in range(DC):
            out_fn(c, xin[:, c, :], rt, gcol[:, c:c + 1])

    def final_out(self, seq):
        S, A = self.S, self.A
        L = seq["L"]
        XS = seq["XS"].rearrange("p (c t) -> p c t", c=DC)
        gcol = self.col("gains", 16, 16 * 3 * self.depth)
        A.top = 0
        xin = A.alloc(F32, DC, 512)
        sqb = A.alloc(BF16, DC, 512)
        rt = A.alloc(F32, 512)
        hf = A.alloc(F32, DC, 512)
        yo = A.alloc(F32, 4, D)
        for t0 in range(0, L, 512):
            self.norm_tile(XS, t0, gcol, xin, sqb, rt,
                           lambda c, xc, r, g: S.stt(hf[:, c, :], xc, g, r, ALU.mult, ALU.mult))
            for k in range(4):
                for cq in range(4):
                    b = self.bank()
                    S.tr([(b[:, 128 * j:128 * (j + 1)], hf[:, 4 * cq + j, 128 * k:128 * (k + 1)]) for j in range(4)],
                         self.ident)
                    S.copy(yo[:, k, 512 * cq:512 * (cq + 1)], b, e=("act" if cq % 2 else "dve"))
            for k in range(4):
                S.dma("sp", seq["y"][t0 + 128 * k:t0 + 128 * (k + 1), :], yo[:, k, :])

    def prep(self, W, nk, col_blocks, dst, bw):
        S, A = self.S, self.A
        A.top = 0
        KH = 16 if nk <= 16 else (nk + 1) // 2
        stg = [A.alloc(F32, KH, 256) for _ in range(2)]
        ob = [A.alloc(BF16, nk, bw) for _ in range(2)]
        ncw = W.shape[1]
        n = 0
        for b, runs in enumerate(col_blocks):
            o = ob[b % 2]
            j0 = 0
            for (c0, w) in runs:
                for k0 in range(0, nk, KH):
                    kn = min(KH, nk - k0)
                    st = stg[n % 2]
                    src = bass.AP(tensor=W.tensor, offset=W.offset + (k0 * 128) * ncw + c0,
                                  ap=[[ncw, 128], [128 * ncw, kn], [1, w]])
                    S.dma("sp", st[:, 0:kn, 0:w], src)
                    S.copy(o[:, k0:k0 + kn, j0:j0 + w], st[:, 0:kn, 0:w], e=("act" if n % 2 else "dve"))
                    n += 1
                j0 += w
            S.flush()
            S.dma("sp", bass.AP(tensor=dst.tensor, offset=dst.offset + b * 128 * nk * bw, ap=[[nk * bw, 128], [1, nk * bw]]),
                  o.rearrange("p k j -> p (k j)"), defer=True)
        S.flush()

    BG_BASE = 155648

    def prep_gen(self, W, nk, col_blocks, dst, bw):
        S = self.S
        ar = Arena(self.A.ap, self.A.nbytes)
        ar.top = self.BG_BASE
        stg = [ar.alloc(F32, 8, 256) for _ in range(2)]
        ob = [ar.alloc(BF16, 8, 256) for _ in range(2)]
        ncw = W.shape[1]
        n = 0
        for b, runs in enumerate(col_blocks):
            j0 = 0
            for (c0, w) in runs:
                for k0 in range(0, nk, 8):
                    kn = min(8, nk - k0)
                    st, o = stg[n % 2], ob[n % 2]
                    src = bass.AP(tensor=W.tensor, offset=W.offset + (k0 * 128) * ncw + c0,
                                  ap=[[ncw, 128], [128 * ncw, kn], [1, w]])
                    S.dma("sp", st[:, 0:kn, 0:w], src)
                    S.copy(o[:, 0:kn, 0:w], st[:, 0:kn, 0:w], e=("act" if n % 2 else "dve"))
                    S.flush()
                    S.dma("sp", bass.AP(tensor=dst.tensor, offset=dst.offset + b * 128 * nk * bw + k0 * bw + j0,
                                        ap=[[nk * bw, 128], [bw, kn], [1, w]]), o[:, 0:kn, 0:w], defer=True)
                    n += 1
                    yield
                j0 += w
        S.flush()

    def ffn_prep_gens(self, which, layer):
        wi = self.I[f"ffn{which}_wi"][layer]
        wo = self.I[f"ffn{which}_wo"][layer]
        WBI, WBO = self.wsets[which % 2]
        yield from self.prep_gen(wi, DC, [[(256 * jb, 256), (DFF + 256 * jb, 256)] for jb in range(FC // 2)], WBI, 512)
        yield from self.prep_gen(wo, FC, [[(256 * ob, 256)] for ob in range(8)], WBO, 256)

    def bg_step(self, n=1):
        for _ in range(n):
            if self.bg is None:
                return
            try:
                next(self.bg)
            except StopIteration:
                self.bg = None

    def wget(self, slot, scr, b, nk, bw):
        src = bass.AP(tensor=scr.tensor, offset=scr.offset + b * 128 * nk * bw, ap=[[nk * bw, 128], [1, nk * bw]])
        self.S.dma("sp", slot.rearrange("p k j -> p (k j)"), src)

    def group_norm(self, seq, t0, T, gcol, h, scratch):
        S = self.S
        XS = seq["XS"].rearrange("p (c t) -> p c t", c=DC)
        xin, sqb, rt = scratch
        xins = xin if isinstance(xin, list) else [xin]
        for s in range(T // 512):
            self.norm_tile(XS, t0 + 512 * s, gcol, xins[s % len(xins)], sqb, rt,
                           lambda c, xc, r, g, s=s: S.stt(h[:, c, 512 * s:512 * (s + 1)], xc, g, r, ALU.mult, ALU.mult))

    def out_proj(self, seq, t0, T, act, nk, W, scale, wslots, xsl, xo):
        S = self.S
        XS = seq["XS"].rearrange("p (c t) -> p c t", c=DC)
        def loads(ob):
            self.wget(wslots[ob % 2], W, ob, nk, 256)
            S.dma("sp", xsl[ob % 2], XS[:, 2 * ob:2 * ob + 2, t0:t0 + T])

        loads(0)
        for ob in range(8):
            if ob + 1 < 8:
                loads(ob + 1)
            S.flush()
            wb = wslots[ob % 2]
            xs_ = xsl[ob % 2]
            xo_ = xo[ob % 2]
            for oo in range(2):
                for s in range(T // 512):
                    b = self.bank()
                    S.mm(b, [(wb[:, kc, 128 * oo:128 * (oo + 1)], act[:, kc, 512 * s:512 * (s + 1)]) for kc in range(nk)])
                    S.stt(xo_[:, oo, 512 * s:512 * (s + 1)], b, float(scale), xs_[:, oo, 512 * s:512 * (s + 1)],
                          ALU.mult, ALU.add)
            S.dma("sp", XS[:, 2 * ob:2 * ob + 2, t0:t0 + T], xo_, defer=True)
        S.flush()

    def ffn(self, which, layer):
        S, A, T = self.S, self.A, self.T
        wi = self.I[f"ffn{which}_wi"][layer]
        wo = self.I[f"ffn{which}_wo"][layer]
        gcol = self.col("gains", 16, 16 * ((0 if which == 1 else 2) * self.depth + layer))
        WBI, WBO = self.wsets[which % 2]
        if (which, layer) in getattr(self, "bg_for", []):
            while (which, layer) not in self.bg_done and self.bg is not None:
                self.bg_step()
            assert (which, layer) in self.bg_done
        else:
            self.prep(wi, DC, [[(256 * jb, 256), (DFF + 256 * jb, 256)] for jb in range(FC // 2)], WBI, 512)
            self.prep(wo, FC, [[(256 * ob, 256)] for ob in range(8)], WBO, 256)
        wo = WBO
        A.top = 0
        h = A.alloc(BF16, DC, T)
        act = A.alloc(BF16, FC, T)
        wtop = A.top
        wslots = [A.alloc(BF16, DC, 2, 256) for _ in range(3)]
        A.top = wtop
        woslots = [A.alloc(BF16, FC, 256) for _ in range(2)]
        A.top = max(A.top, wtop + 3 * 16384)
        sg = [A.alloc(F32, 512) for _ in range(2)]
        rt = A.alloc(F32, 512)
        top = A.top
        A.top = (DC * T * 2)
        xin = [A.alloc(F32, DC, 512) for _ in range(2 if T >= 1024 else 1)]
        sqb = A.alloc(BF16, DC, 512)
        assert T < 1024 or A.top <= (DC + FC) * T * 2
        A.top = 0
        xsl = [A.alloc(F32, 2, T) for _ in range(2)]
        xo = [A.alloc(F32, 2, T) for _ in range(2)]
        assert A.top <= DC * T * 2
        A.top = top
        for seq in self.seqs:
            for t0 in range(0, seq["L"], T):
                self.group_norm(seq, t0, T, gcol, h, (xin, sqb, rt))
                for jb in range(FC // 2):
                    wb = wslots[jb % 3]
                    self.wget(wb.rearrange("p k a j -> p k (a j)"), WBI, jb, DC, 512)
                    for s in range(T // 512):
                        for jj in range(2):
                            gb, ub = self.bank(), self.bank()
                            hs = [h[:, kc, 512 * s:512 * (s + 1)] for kc in range(DC)]
                            S.mm(gb, [(wb[:, kc, 0, 128 * jj:128 * (jj + 1)], hs[kc]) for kc in range(DC)])
                            S.mm(ub, [(wb[:, kc, 1, 128 * jj:128 * (jj + 1)], hs[kc]) for kc in range(DC)])
                            sgt = sg[jj]
                            S.act(sgt, gb, AF.Silu)
                            S.tt(act[:, 2 * jb + jj, 512 * s:512 * (s + 1)], sgt, ub, ALU.mult)
                self.out_proj(seq, t0, T, act, FC, wo, 0.5, woslots, xsl, xo)

    def build_bias(self):
        S, A, I = self.S, self.A, self.I
        A.top = 0
        rb = A.alloc(F32, 16)
        oh = A.alloc(F32, 512)
        mk = A.alloc(F32, 512)
        on1 = A.alloc(F32, 16)
        jr = A.alloc(F32, 128)
        tb = A.alloc(F32, 512)
        hk = A.alloc(F32, 384)
        bt = A.alloc(F32, 16, 384)
        S.dma("sp", rb[0:32, :], I["rel_bias"])
        S.dma("sp", oh[0:32, :], I["oh"])
        S.dma("sp", mk[0:1, :], I["mask"])
        S.dma("sp", jr, I["jrev"])
        S.memset(on1[0:1, :], 1.0)
        b = self.bank()
        S.mm(b[0:16, :], [(rb[0:32, :], oh[0:32, :]), (on1[0:1, :], mk[0:1, :])])
        S.copy(tb[0:16, :], b[0:16, :])
        S.dma("sp", self.TB, tb[0:16, :])
        for hh in range(16):
            src = bass.AP(tensor=self.TB.tensor, offset=self.TB.offset + 512 * hh, ap=[[1, 128], [1, 384]])
            S.dma("sp", hk, src)
            b = self.bank()
            S.mm(b[:, 0:384], [(jr, hk)])
            S.copy(bt[:, hh, :], b[:, 0:384])
        S.dma("sp", self.BIAS.rearrange("p (h k) -> p h k", h=16), bt)

    def mix_odd(self, i, layer):
        S, A, T = self.S, self.A, self.T
        if not getattr(self, "_bias_done", False):
            self.build_bias()
            self._bias_done = True
        wqkv = self.I["attn_w_qkv"][i]
        self.prep(wqkv, DC, [[(256 * b, 256)] for b in range(12)], self.WBM, 256)
        self.prep(self.I["attn_w_o"][i], DC, [[(256 * b, 256)] for b in range(8)], self.WBM2, 256)
        wo = self.WBM2
        gcol = self.col("gains", 16, 16 * (self.depth + layer))
        sink = self.col("sink", 16, 16 * i)
        for seq in self.seqs:
            L = seq["L"]
            nb = L // 128
            QT = self.QT.rearrange("p (c t) -> p c t", c=16)[:, :, 0:L] if False else sv(self.QT, 0, [[L, 16], [1, L]])
            KT = sv(self.KT, 0, [[L, 4], [1, L]])
            VT = sv(self.VT, 0, [[512, nb], [1, 512]])
            OT = sv(self.OT, 0, [[L, 16], [1, L]])
            A.top = 0
            h = A.alloc(BF16, DC, T)
            qk = A.alloc(BF16, 20, T)
            vst = A.alloc(BF16, T // 128, 512)
            wslots = [A.alloc(BF16, DC, 256) for _ in range(3)]
            wv = [A.alloc(BF16, DC, 256) for _ in range(2)]
            xin = A.alloc(F32, DC, 512)
            sqb = A.alloc(BF16, DC, 512)
            rt = A.alloc(F32, 512)
            for t0 in range(0, L, T):
                self.group_norm(seq, t0, T, gcol, h, (xin, sqb, rt))
                for cb in range(10):
                    wb = wslots[cb % 3]
                    self.wget(wb, self.WBM, cb, DC, 256)
                    for jj in range(2):
                        for s in range(T // 512):
                            b = self.bank()
                            S.mm(b, [(wb[:, kc, 128 * jj:128 * (jj + 1)], h[:, kc, 512 * s:512 * (s + 1)]) for kc in range(DC)])
                            S.copy(qk[:, 2 * cb + jj, 512 * s:512 * (s + 1)], b, e=("act" if s % 2 else "dve"))
                self.wget(wv[0], self.WBM, 10, DC, 256)
                self.wget(wv[1], self.WBM, 11, DC, 256)
                for tt in range(T // 128):
                    b = self.bank()
                    for hv in range(2):
                        S.mm(b[:, 256 * hv:256 * (hv + 1)], [(h[:, kc, 128 * tt:128 * (tt + 1)], wv[hv][:, kc, :]) for kc in range(DC)])
                    S.copy(vst[:, tt, :], b, e=("act" if tt % 2 else "dve"))
                S.dma("sp", QT[:, :, t0:t0 + T], qk[:, 0:16, :])
                S.dma("sp", KT[:, :, t0:t0 + T], qk[:, 16:20, :])
                S.dma("sp", VT[:, t0 // 128:(t0 + T) // 128, :], vst)
            A.top = 0
            bias = A.alloc(F32, 16, 384)
            S.dma("sp", bias, self.BIAS.rearrange("p (h k) -> p h k", h=16))
            ktg = A.alloc(BF16, L)
            vtg = A.alloc(BF16, nb, 128)
            QB = min(8, nb)
            qt = A.alloc(BF16, 4, 128 * QB)
            ost = A.alloc(BF16, 4, 128 * QB)
            sc = A.alloc(F32, 4, 384)
            pe_ = A.alloc(F32, 4, 384)
            pb = A.alloc(BF16, 4, 384)
            pT = A.alloc(BF16, 3, 512)
            st = A.alloc(F32, 32)
            isq = 1.0 / math.sqrt(HD)
            for g in range(NKV):
                S.dma("sp", ktg, KT[:, g, :])
                S.dma("sp", vtg, VT[:, :, 128 * g:128 * (g + 1)])
                sk4 = sink[:, 4 * g:4 * g + 4]
                for blk in range(nb):
                    bq = blk % QB
                    if bq == 0:
                        S.dma("sp", qt, QT[:, 4 * g:4 * g + 4, 128 * blk:128 * (blk + QB)])
                    self.bg_step(1)
                    kb0, kb1 = max(blk - 1, 0), min(blk + 1, nb - 1)
                    nkb = kb1 - kb0 + 1
                    W = 128 * nkb
                    boff = 128 * (kb0 - (blk - 1))
                    for h4 in range(4):
                        S.mm(self.bank(h4)[:, 0:W], [(qt[:, h4, 128 * bq:128 * (bq + 1)], ktg[:, 128 * kb0:128 * kb0 + W])])
                    psv = sv(self.ps, 0, [[512, 4], [1, W]])
                    S.stt(sc[:, :, 0:W], psv, isq, bias[:, 4 * g:4 * g + 4, boff:boff + W], ALU.mult, ALU.add)
                    rmax, negm, rsum, es, den, rinv, tmp = (st[:, 4 * k:4 * k + 4] for k in range(7))
                    S.reduce(rmax, sc[:, :, 0:W], ALU.max)
                    S.tt(negm, rmax, sk4, ALU.max)
                    S.ts(negm, negm, -1.0, None, ALU.mult)
                    S.memset(rsum, 0.0)
                    for h4 in range(4):
                        S.act(pe_[:, h4, 0:W], sc[:, h4, 0:W], AF.Exp, bias=negm[:, h4:h4 + 1],
                              accum_out=rsum[:, h4:h4 + 1])
                    S.tt(tmp, negm, sk4, ALU.add)
                    S.act(es, tmp, AF.Exp)
                    S.tt(den, rsum, es, ALU.add)
                    S.recip(rinv, den)
                    S.tt(pb[:, :, 0:W], pe_[:, :, 0:W], sv(rinv, 0, [[1, 4], [0, W]]), ALU.mult)
                    for kb in range(nkb):
                        bb = self.bankb(4 + kb)
                        S.tr([(bb[:, 128 * h4:128 * (h4 + 1)], pb[:, h4, 128 * kb:128 * (kb + 1)]) for h4 in range(4)],
                             self.identb)
                        S.copy(pT[:, kb, :], bb[:, 0:512], e=("act" if kb % 2 else "dve"))
                    ob_ = self.bank(7)
                    S.mm(ob_, [(vtg[:, kb0 + kb, :], pT[:, kb, :]) for kb in range(nkb)])
                    S.copy(sv(ost, 128 * bq, [[128 * QB, 4], [1, 128]]), sv(ob_, 0, [[128, 4], [1, 128]]), e="act")
                    if bq == QB - 1:
                        S.dma("sp", OT[:, 4 * g:4 * g + 4, 128 * (blk - QB + 1):128 * (blk + 1)], ost)
            A.top = 0
            o = A.alloc(BF16, DC, T)
            woslots = [A.alloc(BF16, DC, 256) for _ in range(2)]
            xsl = [A.alloc(F32, 2, T) for _ in range(2)]
            xo = [A.alloc(F32, 2, T) for _ in range(2)]
            for t0 in range(0, L, T):
                S.dma("sp", o, OT[:, :, t0:t0 + T])
                self.out_proj(seq, t0, T, o, DC, wo, 1.0, woslots, xsl, xo)

    def cmul(self, src_bank, tab, dst, k, tP, tQ):
        S = self.S
        a_all = sv(src_bank, 0, [[256, 2], [128, 2], [1, 128]])
        a_re = sv(src_bank, 0, [[256, 2], [1, 128]])
        a_im = sv(src_bank, 128, [[256, 2], [1, 128]])
        S.tt(sv(tP, 0, [[256, 2], [128, 2], [1, 128]]), a_all, sv(tab, 0, [[0, 2], [0, 2], [1, 128]]), ALU.mult)
        S.tt(sv(tQ, 0, [[256, 2], [1, 128]]), a_im, sv(tab, 256, [[0, 2], [1, 128]]), ALU.mult)
        S.tt(sv(tQ, 128, [[256, 2], [1, 128]]), a_re, sv(tab, 128, [[0, 2], [1, 128]]), ALU.mult)
        S.tt(sv(dst, 2 * k * 128, [[128, 2], [512, 2], [1, 128]]), sv(tP, 0, [[256, 2], [128, 2], [1, 128]]),
             sv(tQ, 0, [[256, 2], [128, 2], [1, 128]]), ALU.add)

    def fft_fwd(self, srcT, nK, N2, tb, bufs, sink):
        S = self.S
        Bt, tP, tQ = bufs
        for b in range(N2 // 4):
            for k in range(2):
                bk = self.bank()
                for g2 in range(2):
                    gi = 4 * b + 2 * k + g2
                    S.mm(bk[:, 256 * g2:256 * (g2 + 1)], [(srcT[0:nK, 128 * gi:128 * (gi + 1)], self.f1[0:nK, :])])
                self.cmul(bk, tb["tw1"], Bt, k, tP, tQ)
            bre = Bt[:, 0, :, :].rearrange("p g k -> p (g k)")
            bim = Bt[:, 1, :, :].rearrange("p g k -> p (g k)")
            xre, xim = self.bank(), self.bank()
            S.mm(xre, [(tb["bd"][:, 0, :], bre), (tb["bd"][:, 2, :], bim)])
            S.mm(xim, [(tb["bd"][:, 1, :], bre), (tb["bd"][:, 0, :], bim)])
            sink(b, xre, xim)

    def mix_even(self, i, layer):
        S, A, T, I = self.S, self.A, self.T, self.I
        self.prep(I["ab_w_in"][i], DC, [[(256 * b, 256)] for b in range(16)], self.WBM, 256)
        self.prep(I["ab_w_out"][i], DC, [[(256 * b, 256)] for b in range(8)], self.WBM2, 256)
        pw = I["pool_w"][i]
        pw2 = bass.AP(tensor=pw.tensor, offset=pw.offset, ap=[[256, 1024], [1, 256]])
        self.prep(pw2, 8, [[(0, 256)]], self.WBP, 256)
        gcol = self.col("gains", 16, 16 * (self.depth + layer))
        c_cw = self.cols["convw"][0] + 72 * i
        c_cb = self.cols["convb"][0] + 24 * i
        c_d = self.cols["hyd"][0] + 8 * i
        c_b4 = self.cols["b4"][0] + 16 * i
        c_mlp = self.cols["mlp"][0] + 8 * i
        c_nd = self.cols["ndelta"][0]
        c_ps = self.cols["pscale"][0] + 8 * i
        cv = self.cv
        for seq in self.seqs:
            L = seq["L"]
            tag = seq["name"]
            N2 = L // 64
            U3 = sv(self.U, 0, [[L, 32], [1, L]])
            PB3 = sv(self.PB, 0, [[L, 8], [1, L]])
            YB3 = sv(self.YB, 0, [[L, 8], [1, L]])
            A.top = 0
            h = A.alloc(BF16, DC, T)
            wslots = [A.alloc(BF16, DC, 256) for _ in range(3)]
            ust = [A.alloc(F32, 2, T) for _ in range(2)]
            xin = A.alloc(F32, DC, 512)
            sqb = A.alloc(BF16, DC, 512)
            rt = A.alloc(F32, 512)
            for t0 in range(0, L, T):
                self.group_norm(seq, t0, T, gcol, h, (xin, sqb, rt))
                self.wget(wslots[0], self.WBM, 0, DC, 256)
                for cb in range(16):
                    wb = wslots[cb % 3]
                    if cb + 1 < 16:
                        self.wget(wslots[(cb + 1) % 3], self.WBM, cb + 1, DC, 256)
                    S.flush()
                    us = ust[cb % 2]
                    for jj in range(2):
                        for s_ in range(T // 512):
                            b = self.bank()
                            S.mm(b, [(wb[:, kc, 128 * jj:128 * (jj + 1)], h[:, kc, 512 * s_:512 * (s_ + 1)]) for kc in range(DC)])
                            S.copy(us[:, jj, 512 * s_:512 * (s_ + 1)], b, e=("act" if s_ % 2 else "dve"))
                    S.dma("sp", U3[:, 2 * cb:2 * cb + 2, t0:t0 + T], us, defer=True)
                S.flush()
            A.top = 0
            ub = A.alloc(F32, L + 16)
            t1 = A.alloc(F32, L + 16)
            t2 = A.alloc(F32, L + 16)
            pout = A.alloc(BF16, L)
            ec = A.alloc(F32, 64)
            e8 = A.alloc(F32, 16)
            S.dma("sp", ec, I[f"ec_{tag}"])
            S.memset(ub[:, 0:8], 0.0)
            S.memset(ub[:, L + 8:L + 16], 0.0)

            def R(buf, lo, hi):
                return buf[:, 8 + lo:8 + hi]

            for c in range(8):
                gi = c // 2
                w = POOL_WINDOWS[gi]
                self.bg_step(9)
                S.dma("sp", R(ub, 0, L), U3[:, c, :])
                steps = {2: [(0, -1, 0)], 4: [(1, -1, 0), (0, -1, 1)], 8: [(3, -1, 0), (2, -1, 1), (0, -2, 2)],
                         16: [(7, -1, 0), (6, -1, 1), (4, -2, 2), (0, -4, 4)]}[w]
                src = ub
                dsts = [t1, t2]
                for li, (m, sa, sb) in enumerate(steps):
                    dst = dsts[li % 2]
                    S.tt(R(dst, -m, L + m), R(src, -m + sa, L + m + sa), R(src, -m + sb, L + m + sb), ALU.add)
                    src = dst
                ssum = src
                pf = dsts[len(steps) % 2]
                S.stt(R(pf, 0, L), R(ssum, 0, L), 1.0 / w, R(ub, 0, L), ALU.mult, ALU.subtract)
                for (lo, eo) in ((0, 0), (L - 8, 8)):
                    S.tt(e8[:, eo:eo + 8], R(ssum, lo, lo + 8), ec[:, 16 * gi + eo:16 * gi + eo + 8], ALU.mult)
                    S.tt(R(pf, lo, lo + 8), e8[:, eo:eo + 8], R(ub, lo, lo + 8), ALU.subtract)
                S.copy(pout, R(pf, 0, L), e="act")
                S.dma("sp", PB3[:, c, :], pout)
            A.top = 0
            tb = {}
            tb["tw1"] = A.alloc(F32, 384)
            tb["tw2"] = A.alloc(F32, 384)
            tb["bd"] = A.alloc(BF16, 3, 128)
            tb["cinv"] = A.alloc(BF16, 2, 256)
            S.dma("sp", tb["tw1"], I[f"tw1_{tag}"].rearrange("p a k -> p (a k)"))
            S.dma("sp", tb["tw2"], I[f"tw2_{tag}"].rearrange("p a k -> p (a k)"))
            S.dma("sp", tb["bd"], I[f"bd_{tag}"])
            S.dma("sp", tb["cinv"], I[f"cinv_{tag}"])
            w1 = A.alloc(F32, 64)
            w2 = A.alloc(F32, 64)
            w3 = A.alloc(F32, 64)
            w4 = A.alloc(F32, 2048)
            S.dma("sp", w1[0:33, :], I["hy_ff_w1"][i])
            S.dma("sp", w2[0:64, :], I["hy_ff_w2"][i])
            S.dma("sp", w3[0:64, :], I["hy_ff_w3"][i])
            S.dma("sp", w4[0:64, :], I["hy_ff_w4"][i])
            mark = A.top
            zt = A.alloc(F32, 512)
            ta = A.alloc(F32, 512)
            tn = A.alloc(F32, 512)
            hb = [A.alloc(F32, 512) for _ in range(2)]
            zext = I[f"zext_{tag}"]
            text = I[f"text_{tag}"]
            H3 = self.H3
            for j0 in range(0, 2 * L, 512):
                S.dma("sp", zt[0:33, :], zext[:, j0:j0 + 512])
                rhs = zt[0:33, :]
                for li, (wl, kk) in enumerate(((w1, 33), (w2, 64), (w3, 64))):
                    b = self.bank()
                    S.mm(b[0:64, :], [(wl[0:kk, 0:64], rhs)])
                    S.ts(ta[0:64, :], b[0:64, :], cv[0:64, c_mlp + 3:c_mlp + 4], cv[0:64, c_mlp + 4 + li:c_mlp + 5 + li],
                         ALU.mult, ALU.add)
                    S.ts(tn[0:64, :], ta[0:64, :], 1.0 / (2.0 * math.pi), 12582912.0, ALU.mult, ALU.add)
                    S.ts(tn[0:64, :], tn[0:64, :], -12582912.0, -2.0 * math.pi, ALU.add, ALU.mult)
                    S.tt(ta[0:64, :], ta[0:64, :], tn[0:64, :], ALU.add)
                    S.ts(ta[0:64, :], ta[0:64, :], 3.1415925, -3.1415925, ALU.min, ALU.max)
                    ho = hb[li % 2]
                    S.act(ho[0:64, :], ta[0:64, :], AF.Sin)
                    rhs = ho[0:64, :]
                S.dma("sp", H3[:, j0:j0 + 512], rhs)
            A.top = mark
            ld = A.alloc(F32, L + 2)
            x0c = A.alloc(BF16, L)
            vx = A.alloc(F32, L)
            xT = A.alloc(BF16, 128 * N2)
            yT = bass.AP(tensor=xT.tensor, offset=vx.offset * 2, ap=[list(xT.ap[0]), [1, 128 * N2]])
            Bt = A.alloc(BF16, 2, 4, 128)
            Yt = A.alloc(BF16, 2, 4, 128)
            Gt = A.alloc(BF16, 2, 4, 128)
            tP = A.alloc(F32, 512)
            tQ = A.alloc(F32, 512)
            kft = [A.alloc(F32, 2, 4, 128) for _ in range(2)]
            tm = [A.alloc(F32, 512) for _ in range(4)]
            h3t = A.alloc(F32, 512)
            txt = A.alloc(F32, 512)
            dec = A.alloc(F32, 512)
            ksum = A.alloc(F32, 64)
            rn = A.alloc(F32, 4)
            cbuf = bass.AP(tensor=vx.tensor, offset=(xT.offset // 2), ap=[list(vx.ap[0]), [1, L]])
            kbuf = bass.AP(tensor=xT.tensor, offset=ld.offset * 2, ap=[list(xT.ap[0]), [1, 2 * L]])
            yb = xT[:, 0:L]
            S.memset(ld[:, 0:1], 0.0)
            S.memset(ld[:, L + 1:L + 2], 0.0)
            KFv = self.KF
            nt = (2 * L) // 512

            def conv3(j, out, first_out=None):
                S.memset(ld[:, 0:1], 0.0)
                S.dma("sp", ld[:, 1:L + 1], U3[:, 8 + j, :])
                acc = first_out if first_out is not None else out
                S.ts(acc, ld[:, 1:L + 1], cv[:, c_cw + 24 + j:c_cw + 25 + j], cv[:, c_cb + j:c_cb + j + 1], ALU.mult, ALU.add)
                S.stt(acc, ld[:, 0:L], cv[:, c_cw + j:c_cw + j + 1], acc, ALU.mult, ALU.add)
                S.stt(out, ld[:, 2:L + 2], cv[:, c_cw + 48 + j:c_cw + 49 + j], acc, ALU.mult, ALU.add)

            for c in range(8):
                if A.top <= self.BG_BASE:
                    self.bg_step(16)
                for ti in range(nt):
                    j0 = 512 * ti
                    S.dma("sp", h3t[0:64, :], H3[:, j0:j0 + 512])
                    S.dma("sp", txt, bass.AP(tensor=text.tensor, offset=text.offset + j0, ap=[[0, 128], [1, 512]]))
                    S.act(dec, txt, AF.Exp, scale=cv[:, c_nd + c:c_nd + c + 1])
                    back = j0 >= L
                    wc = 1024 * back + 128 * c
                    b = self.bank()
                    S.mm(b, [(w4[0:64, wc:wc + 128], h3t[0:64, :])])
                    kf32 = tm[ti % 2]
                    S.stt(kf32, b, cv[:, c_b4 + 8 * back + c:c_b4 + 8 * back + c + 1], dec, ALU.add, ALU.mult)
                    if j0 == L:
                        S.memset(kf32[:, 0:1], 0.0)
                    S.reduce(ksum[:, ti:ti + 1], kf32, ALU.add, absval=True)
                    S.copy(kbuf[:, j0:j0 + 512], kf32, e="act")
                S.reduce(rn[:, 0:1], ksum[:, 0:nt], ALU.add)
                S.recip(rn[:, 1:2], rn[:, 0:1])
                for q4 in range(N2 // 4):
                    bb = self.bankb(self.bankrr)
                    self.bankrr = (self.bankrr + 1) % 8
                    S.tr([(bb[:, 128 * q:128 * (q + 1)], sv(kbuf, 4 * q4 + q, [[N2, 128]])) for q in range(4)], self.identb)
                    S.copy(sv(xT, 4 * q4, [[N2, 128], [1, 4]]), sv(bb, 0, [[1, 128], [128, 4]]), e=("act" if q4 % 2 else "dve"))

                def fsink(b, xre, xim):
                    kt = kft[b % 2]
                    S.copy(kt[:, 0, :, :].rearrange("p g k -> p (g k)"), xre, e="act")
                    S.copy(kt[:, 1, :, :].rearrange("p g k -> p (g k)"), xim, e="dve")
                    S.dma("sp", bass.AP(tensor=KFv.tensor, offset=KFv.offset + 1024 * b, ap=[[N2 * 256, 128], [1, 1024]]),
                          kt.rearrange("p a g k -> p (a g k)"))

                self.fft_fwd(xT, 128, N2, tb, (Bt, tP, tQ), fsink)
                conv3(16 + c, vx)
                conv3(8 + c, cbuf)
                S.tt(vx, vx, cbuf, ALU.mult)
                conv3(c, x0c, first_out=cbuf)
                S.ts(ld[:, 1:L + 1], vx, cv[:, c_d + c:c_d + c + 1], 0.0, ALU.mult, ALU.add)
                z1 = ld[:, 1:L + 1]
                for q4 in range(N2 // 4):
                    bk = self.bank()
                    S.tr([(bk[0:64, 128 * q:128 * (q + 1)], sv(vx, 4 * q4 + q, [[N2, 64]])) for q in range(4)], self.ident)
                    S.copy(sv(xT[0:64, :], 4 * q4, [[N2, 128], [1, 4]]), sv(bk[0:64, :], 0, [[1, 128], [128, 4]]),
                           e=("act" if q4 % 2 else "dve"))

                def dsink(b, xre, xim):
                    kt = kft[b % 2]
                    S.dma("sp", kt.rearrange("p a g k -> p (a g k)"),
                          bass.AP(tensor=KFv.tensor, offset=KFv.offset + 1024 * b, ap=[[N2 * 256, 128], [1, 1024]]))
                    kre = kt[:, 0, :, :].rearrange("p g k -> p (g k)")
                    kim = kt[:, 1, :, :].rearrange("p g k -> p (g k)")
                    S.tt(tm[0], xre, kre, ALU.mult)
                    S.tt(tm[1], xim, kim, ALU.mult)
                    S.tt(Yt[:, 0, :, :].rearrange("p g k -> p (g k)"), tm[0], tm[1], ALU.subtract)
                    S.tt(tm[2], xre, kim, ALU.mult)
                    S.tt(tm[3], xim, kre, ALU.mult)
                    S.tt(Yt[:, 1, :, :].rearrange("p g k -> p (g k)"), tm[2], tm[3], ALU.add)
                    for k in range(2):
                        bk = self.bank()
                        for g2 in range(2):
                            gi = 2 * k + g2
                            S.mm(bk[:, 256 * g2:256 * (g2 + 1)], [(Yt[:, 0, gi, :], tb["cinv"][:, 0, :]),
                                                                  (Yt[:, 1, gi, :], tb["cinv"][:, 1, :])])
                        self.cmul(bk, tb["tw2"], Gt, k, tP, tQ)
                    be = self.bank()
                    S.mm(be[0:64, :], [(self.etab[:, 0, :], Gt[:, 0, :, :].rearrange("p g k -> p (g k)")),
                                       (self.etab[:, 1, :], Gt[:, 1, :, :].rearrange("p g k -> p (g k)"))])
                    S.copy(yT[0:64, 512 * b:512 * (b + 1)], be[0:64, :], e="act")

                self.fft_fwd(xT, 64, N2, tb, (Bt, tP, tQ), dsink)
                for j in range(N2 // 8):
                    bb = self.bankb(self.bankrr)
                    self.bankrr = (self.bankrr + 1) % 8
                    S.tr([(bb[:, 64 * q:64 * (q + 1)], sv(yT[0:64, :], 8 * j + q, [[N2, 128]])) for q in range(8)], self.identb)
                    tf = tm[j % 2]
                    S.stt(sv(tf, 0, [[64, 8], [1, 64]]), sv(bb, 0, [[64, 8], [1, 64]]), rn[:, 1:2],
                          sv(z1, 8 * j, [[1, 8], [N2, 64]]), ALU.mult, ALU.add)
                    S.tt(sv(yb, 8 * j, [[1, 8], [N2, 64]]), sv(tf, 0, [[64, 8], [1, 64]]),
                         sv(x0c, 8 * j, [[1, 8], [N2, 64]]), ALU.mult)
                S.dma("sp", YB3[:, c, :], yb)
            A.top = 0
            cat = A.alloc(BF16, DC, T)
            pin = A.alloc(BF16, 8, T)
            wp = A.alloc(BF16, 8, 256)
            woslots = [A.alloc(BF16, DC, 256) for _ in range(2)]
            xsl = [A.alloc(F32, 2, T) for _ in range(2)]
            xo = [A.alloc(F32, 2, T) for _ in range(2)]
            self.wget(wp, self.WBP, 0, 8, 256)
            for t0 in range(0, L, T):
                S.dma("sp", pin, PB3[:, :, t0:t0 + T])
                S.dma("sp", cat[:, 8:16, :], YB3[:, :, t0:t0 + T])
                for g in range(4):
                    for oc in range(2):
                        for s_ in range(T // 512):
                            b = self.bank()
                            S.mm(b, [(wp[:, 2 * g + kc, 128 * oc:128 * (oc + 1)], pin[:, 2 * g + kc, 512 * s_:512 * (s_ + 1)])
                                     for kc in range(2)])
                            cc = c_ps + 2 * g + oc
                            S.ts(cat[:, 2 * g + oc, 512 * s_:512 * (s_ + 1)], b, cv[:, cc:cc + 1], 0.0, ALU.mult, ALU.add)
                self.out_proj(seq, t0, T, cat, DC, self.WBM2, 1.0, woslots, xsl, xo)


def pack_cvec(inp, depth):
    n_even, n_odd = (depth + 1) // 2, depth // 2
    cols = {}
    parts = []
    pos = [0]

    def add(name, arr):
        arr = np.ascontiguousarray(arr, dtype=np.float32)
        cols[name] = (pos[0], arr.shape[1])
        parts.append(arr)
        pos[0] += arr.shape[1]

    g = [fm(inp["norm_ffn1"][l], 16) for l in range(depth)] + [fm(inp["norm_mix"][l], 16) for l in range(depth)] \
        + [fm(inp["norm_ffn2"][l], 16) for l in range(depth)] + [fm(inp["norm_final"], 16)]
    add("gains", np.concatenate(g, 1))
    z = np.zeros((128, 1), np.float32)
    if n_even:
        add("pscale", np.concatenate([fm(inp["pool_scale"][i], 8) for i in range(n_even)], 1))
        add("convw", np.concatenate([fm(inp["hy_conv_w"][i][k], 24) for i in range(n_even) for k in range(3)], 1))
        add("convb", np.concatenate([fm(inp["hy_conv_b"][i], 24) for i in range(n_even)], 1))
        add("hyd", np.concatenate([fm(inp["hy_d"][i], 8) for i in range(n_even)], 1))
        add("b4", np.concatenate([fm(inp["hy_ff_b4"][i], 16) for i in range(n_even)], 1))
        m = np.zeros((128, 8 * n_even), np.float32)
        for i in range(n_even):
            for k, nm in enumerate(("hy_ff_b1", "hy_ff_b2", "hy_ff_b3", "hy_freq")):
                m[0:64, 8 * i + k] = inp[nm][i]
        add("mlp", m)
    else:
        add("mlp", np.zeros((128, 8), np.float32))
    if n_odd:
        add("sink", np.concatenate([np.broadcast_to(np.asarray(inp["attn_sink"][i], np.float32)[None, :], (128, 16))
                                    for i in range(n_odd)], 1))
    mind = math.log(1e-2) / 1.5
    maxd = math.log(1e-2) / 0.3
    deltas = np.linspace(mind, maxd, CH, dtype=np.float32)
    add("ndelta", -np.abs(fm(deltas, 8)))
    return np.concatenate(parts, 1), cols


_CACHE = {}


def kernel(**inputs):
    cfg = dict(CFG)
    cfg.update(inputs.pop("_cfg", {}))
    inp = {k: np.asarray(v) for k, v in inputs.items()}
    depth = cfg["depth"]
    Lp, Ls = cfg["Lp"], cfg["Ls"]
    cvec, cols = pack_cvec(inp, depth)
    ct = common_tables()
    host = dict(cvec=cvec, ident=ct["ident"], identb=ct["identb"], f1=ct["f1"], etab=ct["e"], oh=ct["oh"],
                mask=ct["mask"], jrev=ct["jrev"])
    for tag, L in (("p", Lp), ("s", Ls)):
        ft = fft_tables(L)
        for k in ("tw1", "tw2", "bd", "cinv", "zext", "text"):
            host[f"{k}_{tag}"] = ft[k]
        host[f"ec_{tag}"] = pool_edges(L)
    for k in ("ffn1_wi", "ffn1_wo", "ffn2_wi", "ffn2_wo", "ab_w_in", "ab_w_out", "pool_w", "attn_w_qkv", "attn_w_o",
              "rel_bias", "hy_ff_w1", "hy_ff_w2", "hy_ff_w3", "hy_ff_w4"):
        plan = cfg.get("plan")
        if plan is not None and k.startswith("ffn") and not any(p[0] == "ffn" and f"ffn{p[1]}" == k[:4] for p in plan):
            continue
        if inp[k].size > 0:
            host[k] = np.ascontiguousarray(inp[k], dtype=np.float32)
    xp = np.ascontiguousarray(inp["x_prompt"][0], dtype=np.float32)
    xs = np.asarray(inp["x_sample"], dtype=np.float32)
    nb = xs.shape[0]
    host["xp"] = xp
    host["xs"] = np.ascontiguousarray(xs[0])
    import time as _time
    _t0 = _time.time()
    nc = bass.Bass("TRN2", target_bir_lowering=False)
    prog = Prog(cfg, cols, cvec.shape[1])
    prog.build(nc, host)
    print(f"[kernel] build {_time.time() - _t0:.1f}s ops={prog.S.nops} sems={prog.S.nsem}", flush=True)
    _t0 = _time.time()
    if cfg.get("build_only"):
        return None
    in_maps = []
    for c in range(8):
        m = dict(host)
        m["xs"] = np.ascontiguousarray(xs[c % nb])
        in_maps.append(m)
    res = run_bass_kernel_spmd(nc, in_maps, core_ids=list(range(8)))
    print(f"[kernel] run {_time.time() - _t0:.1f}s", flush=True)
    yp = np.asarray(res.results[0]["yp"], dtype=np.float32)[None]
    ys = np.stack([np.asarray(res.results[c]["ys"], dtype=np.float32) for c in range(nb)], 0)
    return (yp, ys)
```
